# Optimizing a Trainium2 kernel written in Bass

```python
import math
import jax, jax.numpy as jnp
from jax import lax
import numpy as np

D_MODEL = 1024
BATCH = 8
SEQ = 2048
DEPTH = 1

LRU_WIDTH = D_MODEL
LRU_HEADS = 16
LRU_HEAD_DIM = LRU_WIDTH // LRU_HEADS
CONV_WIDTH = 4
LRU_C = 8.0
S5_WIDTH = D_MODEL // 2
S5_GROUP = 16
S5_GROUPS = S5_WIDTH // S5_GROUP
S5_STATE = 64
D_FF = 2816
EPS = 1e-6
N_MOD = 9
IN_COLS = 2 * LRU_WIDTH + S5_WIDTH + 2 * D_MODEL

kernel_name = 'hybrid_rglru_s5_macaron_adaln'


def rms_norm(x, g):
    xf = x.astype(jnp.float32)
    y = xf * lax.rsqrt(jnp.mean(xf * xf, axis=-1, keepdims=True) + EPS)
    return (y * g.astype(jnp.float32)).astype(x.dtype)


def modulate(n, shift, scale):
    return n * (1.0 + scale[:, None, :]) + shift[:, None, :]


def swiglu(u, w_up, w_down):
    a, b = jnp.split(u @ w_up, 2, axis=-1)
    return (jax.nn.silu(a) * b) @ w_down


def causal_depthwise_conv(x, w, b):
    s = x.shape[1]
    xp = jnp.pad(x, ((0, 0), (CONV_WIDTH - 1, 0), (0, 0)))
    y = b
    for k in range(CONV_WIDTH):
        y = y + xp[:, k:k + s, :] * w[k]
    return y


def rg_lru(x, w_r, b_r, w_i, b_i, lam):
    bsz, s, wd = x.shape
    xh = x.reshape(bsz, s, LRU_HEADS, LRU_HEAD_DIM)
    r = jax.nn.sigmoid(jnp.einsum('bshi,hij->bshj', xh, w_r).reshape(bsz, s, wd) + b_r)
    i = jax.nn.sigmoid(jnp.einsum('bshi,hij->bshj', xh, w_i).reshape(bsz, s, wd) + b_i)
    log_a = -LRU_C * r.astype(jnp.float32) * jax.nn.softplus(-lam.astype(jnp.float32))
    a = jnp.exp(log_a)
    mult = jnp.sqrt(-jnp.expm1(2.0 * log_a))
    u = mult * (i * x).astype(jnp.float32)

    def combine(left, right):
        a1, b1 = left
        a2, b2 = right
        return a1 * a2, a2 * b1 + b2

    _, h = lax.associative_scan(combine, (a, u), axis=1)
    return h.astype(x.dtype)


def s5_ssm(x, a_re, a_im, log_dt, b_re, b_im, c_re, c_im, d_skip):
    bsz, s, _ = x.shape
    xf = x.astype(jnp.float32)
    xg = xf.reshape(bsz, s, S5_GROUPS, S5_GROUP)
    a_re = a_re.astype(jnp.float32)
    a_im = a_im.astype(jnp.float32)
    dt = jnp.exp(log_dt.astype(jnp.float32))[:, None]
    mag = jnp.exp(a_re * dt)
    lr = mag * jnp.cos(a_im * dt)
    li = mag * jnp.sin(a_im * dt)
    den = a_re * a_re + a_im * a_im
    nr = lr - 1.0
    cr = (nr * a_re + li * a_im) / den
    ci = (li * a_re - nr * a_im) / den
    bx_re = jnp.einsum('bsgh,gph->bsgp', xg, b_re.astype(jnp.float32))
    bx_im = jnp.einsum('bsgh,gph->bsgp', xg, b_im.astype(jnp.float32))
    u_re = cr * bx_re - ci * bx_im
    u_im = cr * bx_im + ci * bx_re
    lam_re = jnp.broadcast_to(lr, (1, s, S5_GROUPS, S5_STATE))
    lam_im = jnp.broadcast_to(li, (1, s, S5_GROUPS, S5_STATE))

    def combine(left, right):
        ar1, ai1, br1, bi1 = left
        ar2, ai2, br2, bi2 = right
        return (ar2 * ar1 - ai2 * ai1,
                ar2 * ai1 + ai2 * ar1,
                ar2 * br1 - ai2 * bi1 + br2,
                ar2 * bi1 + ai2 * br1 + bi2)

    _, _, h_re, h_im = lax.associative_scan(combine, (lam_re, lam_im, u_re, u_im), axis=1)
    y = (jnp.einsum('bsgp,ghp->bsgh', h_re, c_re.astype(jnp.float32))
         - jnp.einsum('bsgp,ghp->bsgh', h_im, c_im.astype(jnp.float32)))
    y = y.reshape(bsz, s, S5_WIDTH) + d_skip.astype(jnp.float32) * xf
    return y.astype(x.dtype)


def setup_inputs(seed: int = 0) -> dict:
    key = jax.random.key(seed)
    ks = iter(jax.random.split(key, 40))
    L = DEPTH
    f32 = jnp.float32

    def nrm(shape, scale):
        return scale * jax.random.normal(next(ks), shape, f32)

    def gain(shape):
        return 1.0 + nrm(shape, 0.02)

    x = nrm((BATCH, SEQ, D_MODEL), 1.0)
    c = nrm((BATCH, D_MODEL), 1.0)
    mod_w = nrm((L, D_MODEL, N_MOD * D_MODEL), 0.3 * D_MODEL ** -0.5)
    mod_b = nrm((L, N_MOD * D_MODEL), 0.02)
    norm1_g = gain((L, D_MODEL))
    ffn1_w_up = nrm((L, D_MODEL, 2 * D_FF), D_MODEL ** -0.5)
    ffn1_w_down = nrm((L, D_FF, D_MODEL), D_FF ** -0.5)
    norm2_g = gain((L, D_MODEL))
    w_in = nrm((L, D_MODEL, IN_COLS), D_MODEL ** -0.5)
    b_in = nrm((L, IN_COLS), 0.02)
    conv_w = nrm((L, CONV_WIDTH, LRU_WIDTH), CONV_WIDTH ** -0.5)
    conv_b = nrm((L, LRU_WIDTH), 0.02)
    lru_w_r = nrm((L, LRU_HEADS, LRU_HEAD_DIM, LRU_HEAD_DIM), LRU_HEAD_DIM ** -0.5)
    lru_b_r = nrm((L, LRU_WIDTH), 0.02)
    lru_w_i = nrm((L, LRU_HEADS, LRU_HEAD_DIM, LRU_HEAD_DIM), LRU_HEAD_DIM ** -0.5)
    lru_b_i = nrm((L, LRU_WIDTH), 0.02)
    a0 = jax.random.uniform(next(ks), (L, LRU_WIDTH), f32, minval=0.9, maxval=0.999)
    base = a0 ** (1.0 / LRU_C)
    lru_lambda = jnp.log(base) - jnp.log1p(-base)
    proj_a = nrm((L, LRU_WIDTH, D_MODEL), LRU_WIDTH ** -0.5)
    s5_a_re = -0.5 + nrm((L, S5_GROUPS, S5_STATE), 0.01)
    s5_a_im = math.pi * jnp.arange(S5_STATE, dtype=f32)[None, None, :] + nrm((L, S5_GROUPS, S5_STATE), 0.01)
    s5_log_dt = jax.random.uniform(next(ks), (L, S5_GROUPS), f32, minval=math.log(1e-3), maxval=math.log(1e-1))
    s5_b_re = nrm((L, S5_GROUPS, S5_STATE, S5_GROUP), (2.0 * S5_GROUP) ** -0.5)
    s5_b_im = nrm((L, S5_GROUPS, S5_STATE, S5_GROUP), (2.0 * S5_GROUP) ** -0.5)
    s5_c_re = nrm((L, S5_GROUPS, S5_GROUP, S5_STATE), (2.0 * S5_STATE) ** -0.5)
    s5_c_im = nrm((L, S5_GROUPS, S5_GROUP, S5_STATE), (2.0 * S5_STATE) ** -0.5)
    s5_d = nrm((L, S5_WIDTH), 1.0)
    glu_w = nrm((L, S5_WIDTH, S5_WIDTH), S5_WIDTH ** -0.5)
    glu_b = nrm((L, S5_WIDTH), 0.02)
    proj_b = nrm((L, S5_WIDTH, D_MODEL), S5_WIDTH ** -0.5)
    w_out = nrm((L, D_MODEL, D_MODEL), D_MODEL ** -0.5)
    norm3_g = gain((L, D_MODEL))
    ffn2_w_up = nrm((L, D_MODEL, 2 * D_FF), D_MODEL ** -0.5)
    ffn2_w_down = nrm((L, D_FF, D_MODEL), D_FF ** -0.5)
    final_g = gain((D_MODEL,))
    return {'x': x, 'c': c, 'mod_w': mod_w, 'mod_b': mod_b,
            'norm1_g': norm1_g, 'ffn1_w_up': ffn1_w_up, 'ffn1_w_down': ffn1_w_down,
            'norm2_g': norm2_g, 'w_in': w_in, 'b_in': b_in,
            'conv_w': conv_w, 'conv_b': conv_b,
            'lru_w_r': lru_w_r, 'lru_b_r': lru_b_r, 'lru_w_i': lru_w_i, 'lru_b_i': lru_b_i,
            'lru_lambda': lru_lambda, 'proj_a': proj_a,
            's5_a_re': s5_a_re, 's5_a_im': s5_a_im, 's5_log_dt': s5_log_dt,
            's5_b_re': s5_b_re, 's5_b_im': s5_b_im, 's5_c_re': s5_c_re, 's5_c_im': s5_c_im,
            's5_d': s5_d, 'glu_w': glu_w, 'glu_b': glu_b, 'proj_b': proj_b,
            'w_out': w_out, 'norm3_g': norm3_g, 'ffn2_w_up': ffn2_w_up, 'ffn2_w_down': ffn2_w_down,
            'final_g': final_g}


def reference(x, c, mod_w, mod_b, norm1_g, ffn1_w_up, ffn1_w_down, norm2_g, w_in, b_in,
              conv_w, conv_b, lru_w_r, lru_b_r, lru_w_i, lru_b_i, lru_lambda, proj_a,
              s5_a_re, s5_a_im, s5_log_dt, s5_b_re, s5_b_im, s5_c_re, s5_c_im,
              s5_d, glu_w, glu_b, proj_b, w_out, norm3_g, ffn2_w_up, ffn2_w_down, final_g):
    c_act = jax.nn.silu(c)
    split_pts = [LRU_WIDTH, 2 * LRU_WIDTH, 2 * LRU_WIDTH + S5_WIDTH, 2 * LRU_WIDTH + S5_WIDTH + D_MODEL]
    for l in range(DEPTH):
        mod = c_act @ mod_w[l] + mod_b[l]
        sh1, sc1, g1, sh2, sc2, g2, sh3, sc3, g3 = jnp.split(mod, N_MOD, axis=-1)

        u = modulate(rms_norm(x, norm1_g[l]), sh1, sc1)
        x = x + 0.5 * g1[:, None, :] * swiglu(u, ffn1_w_up[l], ffn1_w_down[l])

        u = modulate(rms_norm(x, norm2_g[l]), sh2, sc2)
        z = u @ w_in[l] + b_in[l]
        xa, ga, xb, mg_a, mg_b = jnp.split(z, split_pts, axis=-1)
        xa = causal_depthwise_conv(xa, conv_w[l], conv_b[l])
        ya = rg_lru(xa, lru_w_r[l], lru_b_r[l], lru_w_i[l], lru_b_i[l], lru_lambda[l]) * jax.nn.gelu(ga)
        yb = jax.nn.gelu(s5_ssm(xb, s5_a_re[l], s5_a_im[l], s5_log_dt[l], s5_b_re[l], s5_b_im[l],
                                s5_c_re[l], s5_c_im[l], s5_d[l]))
        yb = yb * jax.nn.sigmoid(yb @ glu_w[l] + glu_b[l])
        m = jax.nn.sigmoid(mg_a) * (ya @ proj_a[l]) + jax.nn.sigmoid(mg_b) * (yb @ proj_b[l])
        x = x + g2[:, None, :] * (m @ w_out[l])

        u = modulate(rms_norm(x, norm3_g[l]), sh3, sc3)
        x = x + 0.5 * g3[:, None, :] * swiglu(u, ffn2_w_up[l], ffn2_w_down[l])
    return rms_norm(x, final_g)
```

```python
import contextlib
import math
import numpy as np
import concourse.bass as bass
import concourse.mybir as mybir
from concourse.bass_utils import run_bass_kernel_spmd

F32 = mybir.dt.float32
BF16 = mybir.dt.bfloat16
AF = mybir.ActivationFunctionType
ALU = mybir.AluOpType

ENGS = ("pe", "dve", "act", "pool", "sp")
D = 1024
SEQ = 2048
HALF = 1024
DFF = 2816
NKF = 22
EPS = 1e-6
STOP = 100
USE_BARRIERS = False
SCUT = 100

V_MODB, V_N1G, V_N2G, V_N3G, V_FG, V_BIN, V_CONVW, V_CONVB, V_LBR, V_LBI, V_LAM, V_GLUB, V_EPS, V_ONE, V_NPI = (
    0, 72, 80, 88, 96, 104, 140, 172, 180, 188, 196, 204, 208, 209, 210)
NV = 212
C_MOD, C_V1, C_V2, C_V3, C_G1H, C_G3H, C_C1, C_TMP, C_C2 = 0, 72, 80, 88, 96, 104, 112, 120, 128
NCV = 160


class T:
    __slots__ = ("w", "rs", "name", "parents", "entry")

    def __init__(self, name=""):
        self.w = None
        self.rs = []
        self.name = name
        self.parents = []
        self.entry = None


class Op:
    __slots__ = ("eng", "fn", "deps", "kind", "sig", "cnt", "dsem", "dcnt", "waits")

    def __init__(self, eng, fn, kind):
        self.eng = eng
        self.fn = fn
        self.kind = kind
        self.deps = []
        self.sig = False
        self.cnt = 0
        self.dsem = None
        self.dcnt = 0
        self.waits = []


class Prog:
    RINGS = {0: (0, 16), 1: (16, 8), 2: (24, 8)}

    def __init__(self, nc, n_dma_sems=32):
        self.nc = nc
        self.ops = {e: [] for e in ENGS}
        self.n_dma_sems = n_dma_sems
        self.dma_counts = [0] * n_dma_sems
        self.last_dma = [None] * n_dma_sems
        self.ring_pos = {k: 0 for k in self.RINGS}
        self.bar = {e: [] for e in ENGS}
        self.deferq = None
        self.atomic_ts = set()

    def barrier(self, force=False):
        if not (USE_BARRIERS or force):
            return
        deps = []
        for e in ENGS:
            for op in reversed(self.ops[e]):
                if op.kind == "c":
                    deps.append(op)
                    break
        for d in self.last_dma:
            if d is not None:
                deps.append(d)
        for e in ENGS:
            self.bar[e] = list(deps)

    def _add(self, eng, fn, reads, writes, kind="c", dsem=None):
        op = Op(eng, fn, kind)
        deps = list(self.bar[eng])
        self.bar[eng] = []
        for t in reads:
            if t.w is not None:
                deps.append(t.w)
            for p in t.parents:
                if p.w is not None:
                    deps.append(p.w)
                deps.extend(p.rs)
        for t in writes:
            if t.w is not None:
                deps.append(t.w)
            deps.extend(t.rs)
            for p in t.parents:
                if p.w is not None:
                    deps.append(p.w)
                deps.extend(p.rs)
            t.parents = []
        seen = set()
        for d in deps:
            if id(d) not in seen:
                seen.add(id(d))
                op.deps.append(d)
        for t in reads:
            t.rs.append(op)
        for t in writes:
            t.w = op
            t.rs = []
        self.ops[eng].append(op)
        if kind == "d":
            base, size = self.RINGS[dsem]
            si = base + self.ring_pos[dsem] % size
            self.ring_pos[dsem] += 1
            prev = self.last_dma[si]
            if prev is not None and prev not in op.deps:
                op.deps.append(prev)
            op.dsem = si
            self.dma_counts[si] += 16
            op.dcnt = self.dma_counts[si]
            self.last_dma[si] = op
        return op

    def c(self, eng, fn, reads=(), writes=()):
        if self.deferq is not None:
            self.deferq.append((eng, fn, list(reads), list(writes), "c", None))
            return None
        return self._add(eng, fn, list(reads), list(writes))

    def dma(self, out_ap, in_ap, reads=(), writes=(), q="sp", sem=0):
        def fn(e, out_ap=out_ap, in_ap=in_ap):
            return e.dma_start(out=out_ap, in_=in_ap)
        if self.deferq is not None:
            self.deferq.append((q, fn, list(reads), list(writes), "d", sem))
            return None
        return self._add(q, fn, list(reads), list(writes), kind="d", dsem=sem)

    def flush(self, q, n):
        cnt = 0
        open_ = set()
        while q and (cnt < n or open_):
            eng, fn, rd, wr, kind, sem = q.pop(0)
            self._add(eng, fn, rd, wr, kind=kind, dsem=sem)
            cnt += 1
            for t in rd:
                open_.discard(id(t))
            for t in wr:
                if id(t) in self.atomic_ts:
                    open_.add(id(t))

    def emit(self):
        nc = self.nc
        for e in ENGS:
            for op in self.ops[e]:
                for d in op.deps:
                    if d.kind == "c":
                        d.sig = True
        for e in ENGS:
            n = 0
            for op in self.ops[e]:
                if op.kind == "c" and op.sig:
                    n += 1
                    op.cnt = n
        for e in ENGS:
            seen = {}
            for op in self.ops[e]:
                need = {}
                for d in op.deps:
                    if d.kind == "c":
                        key = ("c", d.eng)
                        val = d.cnt
                    else:
                        key = ("d", d.dsem)
                        val = d.dcnt
                    if val > need.get(key, 0):
                        need[key] = val
                for key, val in need.items():
                    if seen.get(key, 0) >= val:
                        continue
                    seen[key] = val
                    op.waits.append((key, val))
        with contextlib.ExitStack() as st:
            csem = {e: st.enter_context(nc.semaphore("c_" + e)) for e in ENGS}
            dsem = [st.enter_context(nc.semaphore("d_%d" % i)) for i in range(self.n_dma_sems)]
            block = st.enter_context(nc.Block())

            def run(eng_name):
                def body(e):
                    for op in self.ops[eng_name]:
                        for key, val in op.waits:
                            s = csem[key[1]] if key[0] == "c" else dsem[key[1]]
                            e.wait_ge(s, val)
                        ins = op.fn(e)
                        if op.kind == "d":
                            ins.then_inc(dsem[op.dsem], 16)
                        elif op.sig:
                            ins.then_inc(csem[eng_name], 1)
                    if eng_name == "sp":
                        for i in range(self.n_dma_sems):
                            if self.dma_counts[i] > 0:
                                e.wait_ge(dsem[i], self.dma_counts[i])
                return body

            block.tensor(run("pe"))
            block.vector(run("dve"))
            block.scalar(run("act"))
            block.gpsimd(run("pool"))
            block.sync(run("sp"))


class Arena:
    def __init__(self, ap):
        self.ap = ap
        self.n = ap.shape[1]
        self.off = 0
        self.takes = []

    def reset(self, off=0):
        self.off = off

    def take(self, free_shape, dt, name=""):
        nel = int(np.prod(free_shape))
        nwords = nel if dt == F32 else (nel + 1) // 2
        nwords = (nwords + 7) // 8 * 8
        assert self.off + nwords <= self.n, ("arena overflow", name, self.off, nwords, self.n)
        v = self.ap[:, self.off:self.off + nwords]
        start, end = self.off, self.off + nwords
        self.off += nwords
        parents = []
        keep = []
        for ent in self.takes:
            if ent[0] < end and start < ent[1]:
                parents.extend(ent[2])
                if start <= ent[0] and ent[1] <= end:
                    continue
            keep.append(ent)
        self.takes = keep
        if dt != F32:
            v = v.bitcast(dt)
        v = v[:, 0:nel]
        if len(free_shape) == 2:
            v = v.rearrange("p (a b) -> p a b", b=free_shape[1])
        elif len(free_shape) == 3:
            v = v.rearrange("p (a b c) -> p a b c", b=free_shape[1], c=free_shape[2])
        elif len(free_shape) == 4:
            v = v.rearrange("p (a b c d) -> p a b c d", b=free_shape[1], c=free_shape[2], d=free_shape[3])
        t = T(name)
        t.parents = list(parents)
        t.entry = [start, end, [t], parents]
        self.takes.append(t.entry)
        return v, t

    def child(self, t, name=""):
        c = T(name)
        c.parents = list(t.entry[3])
        c.entry = t.entry
        t.entry[2].append(c)
        return c


def bc(ap, n):
    return bass.AP(ap.tensor, ap.offset, [list(x) for x in ap.ap] + [[0, n]])


def bcmid(ap, n):
    l = [list(x) for x in ap.ap]
    return bass.AP(ap.tensor, ap.offset, [l[0], [0, n]] + l[1:])


def build_nc():
    nc = bass.Bass("TRN2", target_bir_lowering=False)
    dr = {}

    def din(name, shape):
        dr[name] = nc.dram_tensor(name, list(shape), F32, kind="ExternalInput").ap()
        return dr[name]

    xin = din("xin", [SEQ, D])
    cvec = din("cvec", [128, 8])
    modw = din("modw", [D, 9 * D])
    vecs = din("vecs", [128, NV])
    wup1 = din("wup1", [D, 2 * DFF])
    wdn1 = din("wdn1", [DFF, D])
    wup2 = din("wup2", [D, 2 * DFF])
    wdn2 = din("wdn2", [DFF, D])
    win = din("win", [D, 4608])
    wrbd = din("wrbd", [128, 8, 128])
    wibd = din("wibd", [128, 8, 128])
    pja_d = din("pja", [D, D])
    pjb_d = din("pjb", [512, D])
    wo_d = din("wo", [D, D])
    gluw_d = din("gluw", [512, 512])
    s5sc_d = din("s5sc", [128, 3, 16])
    s5b_d = din("s5b", [128, 2, 16, 16])
    s5c_d = din("s5c", [128, 2, 16, 16])
    dmat_d = din("dmat", [128, 32, 128])
    cmask_d = din("cmask", [128, 128])
    bxb_d = din("bxb", [128, 512])
    ident_d = din("ident", [128, 128])
    onesm_d = din("onesm", [128, 128])
    out = nc.dram_tensor("out", [SEQ, D], F32, kind="ExternalOutput").ap()

    with contextlib.ExitStack() as st:
        def sb(name, shape, dt):
            return st.enter_context(nc.sbuf_tensor(name, shape, dt))

        P = Prog(nc)
        X = sb("X", [128, 8, HALF], F32); TX = [T("X%d" % n) for n in range(8)]
        U = sb("U", [128, 8, HALF], BF16); TU = T("U")
        vec = sb("vec", [128, NV], F32); Tvec = T("vec")
        cv = sb("cv", [128, NCV], F32); Tcv = T("cv")
        ident = sb("identf", [128, 128], F32); Tid = T("id")
        identb = sb("identb", [128, 128], BF16); Tidb = T("idb")
        onesm = sb("onesm_s", [128, 128], BF16); Tones = T("ones")
        cmask = sb("cmask_s", [128, 128], F32); Tcm = T("cmask")
        bxb = sb("bxb_s", [128, 512], F32); Tbxb = T("bxb")
        cdiag = sb("cdiag", [128, 8, 4, 128], BF16); Tcd = T("cdiag")
        WR = sb("WR", [128, 8, 128], BF16); WI = sb("WI", [128, 8, 128], BF16); Twri = T("wri")
        WSre = sb("WSre", [128, 16, 128], BF16); WSim = sb("WSim", [128, 16, 128], BF16); Tws = T("ws")
        TK = sb("TK", [128, 32, 128], BF16); Ttk = T("tk")
        Fre = sb("Fre", [128, 16, 128], BF16); Fmim = sb("Fmim", [128, 16, 128], BF16); Tff = T("ff")
        MU = sb("MU", [128, 7, 3, 16], F32); Tmu = T("mu")
        xatail = sb("xatail", [128, 8, 4], BF16); Txat = T("xatail")
        hcar = sb("hcar", [128, 8], F32); Thc = T("hcar")
        Hin = sb("Hin", [128, 2, 16], F32); Thin = T("hin")
        cactb = sb("cactb", [128, 8], BF16); Tcact = T("cact")
        arena_t = sb("arena", [128, 29952], F32)
        AR = Arena(arena_t[:])

        banks = []
        for i in range(8):
            pt = st.enter_context(nc.psum_tensor("bank%d" % i, [128, 512], F32))
            banks.append((pt[:], T("bank%d" % i)))
        bank_i = [0]
        P.atomic_ts = set(id(t) for (_, t) in banks)

        bank_skip = set()

        def nb():
            while (bank_i[0] % 8) in bank_skip:
                bank_i[0] += 1
            b = banks[bank_i[0] % 8]
            bank_i[0] += 1
            return b

        def vcol(off, n=None):
            return vec[:, off:off + 1] if n is None else vec[:, off + n:off + n + 1]

        def ccol(off, n):
            return cv[:, off + n:off + n + 1]

        P.dma(vec[:], vecs, writes=[Tvec], q="sp", sem=1)
        P.dma(ident[:], ident_d, writes=[Tid], q="sp", sem=1)
        P.dma(cmask[:], cmask_d, writes=[Tcm], q="sp", sem=1)
        P.dma(bxb[:], bxb_d, writes=[Tbxb], q="sp", sem=1)
        P.dma(identb[:], ident_d, writes=[Tidb], q="pool", sem=0)
        P.dma(onesm[:], onesm_d, writes=[Tones], q="pool", sem=0)
        P.dma(WR[:], wrbd, writes=[Twri], q="pool", sem=0)
        P.dma(WI[:], wibd, writes=[Twri], q="pool", sem=0)
        P.c("dve", lambda e: e.memset(xatail[:], 0.0), [], [Txat])
        P.c("dve", lambda e: e.memset(hcar[:], 0.0), [], [Thc])
        P.c("dve", lambda e: e.memset(Hin[:], 0.0), [], [Thin])

        AR.reset()
        cin, Tcin = AR.take([8], F32, "cin")
        P.dma(cin, cvec, writes=[Tcin], q="sp", sem=1)
        P.c("act", lambda e: e.activation(out=cactb[:], in_=cin, func=AF.Silu), [Tcin], [Tcact])
        mbufs0 = [AR.take([8, 256], BF16, "mw%d" % i) for i in range(3)]
        pmod, Tpmod = nb()
        bank_skip.add((bank_i[0] - 1) % 8)
        modw_v = modw.rearrange("(kt kp) c -> kp kt c", kp=128)
        mstate = {"bufs": mbufs0, "i": 0}

        def mod_dma(g):
            mb, Tmb = mstate["bufs"][mstate["i"] % len(mstate["bufs"])]
            mstate["i"] += 1
            P.dma(mb, modw_v[:, :, g * 256:(g + 1) * 256], writes=[Tmb], q="pool", sem=0)
            return (g, mb, Tmb)

        def mod_mm(ent):
            g, mb, Tmb = ent

            def fn(e, mb=mb, g=g):
                ins = None
                for tl in range(2):
                    t = g * 2 + tl
                    for kt in range(8):
                        ins = e.matmul(pmod[:, t:t + 1], lhsT=mb[:, kt, tl * 128:(tl + 1) * 128],
                                       rhs=cactb[:, kt:kt + 1], start=(kt == 0), stop=(kt == 7))
                return ins
            P.c("pe", fn, [Tmb, Tcact], [Tpmod])

        def mod_finish(m):
            P.c("dve", lambda e, m=m: e.tensor_tensor(out=cv[:, 8 * m:8 * m + 8], in0=pmod[:, 8 * m:8 * m + 8],
                                                      in1=vec[:, V_MODB + 8 * m:V_MODB + 8 * m + 8], op=ALU.add), [Tpmod, Tvec], [Tcv])
            der = {1: (C_V1, V_N1G), 4: (C_V2, V_N2G), 7: (C_V3, V_N3G)}
            if m in der:
                coff, goff = der[m]
                P.c("dve", lambda e, coff=coff, m=m: e.tensor_scalar(out=cv[:, coff:coff + 8], in0=cv[:, 8 * m:8 * m + 8],
                                                                      scalar1=1.0, scalar2=None, op0=ALU.add), [Tcv], [Tcv])
                P.c("dve", lambda e, coff=coff, goff=goff: e.tensor_tensor(out=cv[:, coff:coff + 8], in0=cv[:, coff:coff + 8],
                                                                            in1=vec[:, goff:goff + 8], op=ALU.mult), [Tcv, Tvec], [Tcv])
            if m == 2:
                P.c("dve", lambda e: e.tensor_scalar(out=cv[:, C_G1H:C_G1H + 8], in0=cv[:, 16:24], scalar1=0.5, scalar2=None, op0=ALU.mult), [Tcv], [Tcv])
            if m == 8:
                P.c("dve", lambda e: e.tensor_scalar(out=cv[:, C_G3H:C_G3H + 8], in0=cv[:, 64:72], scalar1=0.5, scalar2=None, op0=ALU.mult), [Tcv], [Tcv])
                bank_skip.clear()

        ents = [mod_dma(g) for g in range(3)]

        def mod_first():
            for g in range(8):
                mod_mm(ents[g])
                if g + 3 < 8:
                    ents.append(mod_dma(g + 3))
        S_ops = []
        s_tail_start = 0
        s_tail = [0]
        s_cw = [0]
        s_tail_len = [0]
        S_OFF = 17152

        def startup_rest():
            for g in range(8, 36):
                mod_mm(ents[g])
                if g + 3 < 36:
                    ents.append(mod_dma(g + 3))
            for m in range(2, 9):
                mod_finish(m)
        def phase_S():
            P.c("act", lambda e: e.activation(out=cv[:, C_TMP:C_TMP + 8], in_=vec[:, V_LAM:V_LAM + 8], func=AF.Exp, scale=-1.0), [Tvec], [Tcv])
            P.c("act", lambda e: e.activation(out=cv[:, C_TMP:C_TMP + 8], in_=cv[:, C_TMP:C_TMP + 8], func=AF.Ln, bias=vcol(V_ONE)), [Tcv, Tvec], [Tcv])
            P.c("dve", lambda e: e.tensor_scalar(out=cv[:, C_C1:C_C1 + 8], in0=cv[:, C_TMP:C_TMP + 8], scalar1=-8.0, scalar2=None, op0=ALU.mult), [Tcv], [Tcv])
            P.c("dve", lambda e: e.tensor_scalar(out=cv[:, C_C2:C_C2 + 8], in0=cv[:, C_TMP:C_TMP + 8], scalar1=-16.0, scalar2=None, op0=ALU.mult), [Tcv], [Tcv])
            for q in range(8):
                for k in range(4):
                    P.c("dve", lambda e, q=q, k=k: e.tensor_scalar(out=cdiag[:, q, k, :], in0=identb[:], scalar1=vcol(V_CONVW, q * 4 + k),
                                                                   scalar2=None, op0=ALU.mult), [Tidb, Tvec], [Tcd])
            AR.reset(S_OFF)
            NS = 104
            scs, Tsc = AR.take([NS, 16], F32, "scs")
            sidx = [0]

            def snew():
                i = sidx[0]
                sidx[0] += 1
                assert i < NS
                return scs[:, i, :]
            s5in, Ts5in = AR.take([3, 16], F32, "s5in")
            Bt, TBt = AR.take([2, 16, 16], F32, "Bt")
            Ct, TCt = AR.take([2, 16, 16], F32, "Ct")
            P.dma(s5in, s5sc_d, writes=[Ts5in], q="sp", sem=1)
            P.dma(Bt, s5b_d, writes=[TBt], q="sp", sem=1)
            P.dma(Ct, s5c_d, writes=[TCt], q="sp", sem=1)
            P.dma(TK[:], dmat_d, writes=[Ttk], q="pool", sem=0)
            TSD = {}

            def tof(ap):
                key = (ap.tensor.name, int(ap.offset))
                if key not in TSD:
                    if ap.tensor.name == "arena":
                        off = int(ap.offset)
                        ent = [en for en in AR.takes if en[0] <= off < en[1]]
                        assert len(ent) == 1
                        TSD[key] = AR.child(ent[0][2][0], "s%d" % len(TSD))
                    else:
                        TSD[key] = T("s%d" % len(TSD))
                return TSD[key]

            def tt(o, a, b, op):
                P.c("dve", lambda e: e.tensor_tensor(out=o, in0=a, in1=b, op=op), [tof(a), tof(b), Ts5in], [tof(o)])

            def ts(o, a, s1, op0, s2=None, op1=None):
                if op1 is None:
                    P.c("dve", lambda e: e.tensor_scalar(out=o, in0=a, scalar1=s1, scalar2=None, op0=op0), [tof(a), Ts5in], [tof(o)])
                else:
                    P.c("dve", lambda e: e.tensor_scalar(out=o, in0=a, scalar1=s1, scalar2=s2, op0=op0, op1=op1), [tof(a), Ts5in], [tof(o)])

            def act(o, a, func, scale=1.0):
                P.c("act", lambda e: e.activation(out=o, in_=a, func=func, scale=scale), [tof(a), Ts5in], [tof(o)])
            tq = [snew() for _ in range(8)]
            cmi = [0]

            def cmul(oR, oI, aR, aI, bR, bI):
                ta, tb, tc_, td_ = tq[4 * (cmi[0] % 2):4 * (cmi[0] % 2) + 4]
                cmi[0] += 1
                tt(ta, aR, bR, ALU.mult)
                tt(tb, aI, bI, ALU.mult)
                tt(tc_, aR, bI, ALU.mult)
                tt(td_, aI, bR, ALU.mult)
                tt(oR, ta, tb, ALU.subtract)
                tt(oI, tc_, td_, ALU.add)
            ARt, AIt, LDT = s5in[:, 0, :], s5in[:, 1, :], s5in[:, 2, :]
            DT = snew(); act(DT, LDT, AF.Exp)
            XR = snew(); tt(XR, ARt, DT, ALU.mult)
            XI = snew(); tt(XI, AIt, DT, ALU.mult)
            MAG = snew(); act(MAG, XR, AF.Exp)
            sn = snew(); act(sn, XI, AF.Sin, scale=0.125)
            hs = snew(); act(hs, XI, AF.Sin, scale=0.0625)
            cs = snew(); tt(cs, hs, hs, ALU.mult); ts(cs, cs, -2.0, ALU.mult, 1.0, ALU.add)
            for _ in range(3):
                t1 = snew(); t2 = snew()
                tt(t1, cs, cs, ALU.mult)
                tt(t2, sn, sn, ALU.mult)
                P.c("dve", lambda e, sn=sn, cs=cs: e.scalar_tensor_tensor(out=sn, in0=sn, scalar=2.0, in1=cs, op0=ALU.mult, op1=ALU.mult), [tof(sn), tof(cs)], [tof(sn)])
                tt(cs, t1, t2, ALU.subtract)
                sidx[0] -= 2
            LR = snew(); tt(LR, MAG, cs, ALU.mult)
            LI = snew(); tt(LI, MAG, sn, ALU.mult)
            den = snew(); t1 = snew()
            tt(den, ARt, ARt, ALU.mult); tt(t1, AIt, AIt, ALU.mult); tt(den, den, t1, ALU.add)
            P.c("dve", lambda e: e.reciprocal(out=den, in_=den), [tof(den)], [tof(den)])
            NR = snew(); ts(NR, LR, -1.0, ALU.add)
            CR = snew(); CI = snew()
            tt(CR, NR, ARt, ALU.mult); tt(t1, LI, AIt, ALU.mult); tt(CR, CR, t1, ALU.add); tt(CR, CR, den, ALU.mult)
            tt(CI, LI, ARt, ALU.mult); tt(t1, NR, AIt, ALU.mult); tt(CI, CI, t1, ALU.subtract); tt(CI, CI, den, ALU.mult)
            PRk = [snew() for _ in range(9)]; PIk = [snew() for _ in range(9)]
            P.c("dve", lambda e: e.memset(PRk[0], 1.0), [], [tof(PRk[0])])
            P.c("dve", lambda e: e.memset(PIk[0], 0.0), [], [tof(PIk[0])])
            for k in range(8):
                cmul(PRk[k + 1], PIk[k + 1], PRk[k], PIk[k], LR, LI)
            m2 = snew(); tt(m2, LR, LR, ALU.mult); tt(t1, LI, LI, ALU.mult); tt(m2, m2, t1, ALU.add)
            P.c("dve", lambda e: e.reciprocal(out=m2, in_=m2), [tof(m2)], [tof(m2)])
            IR = snew(); II = snew()
            tt(IR, LR, m2, ALU.mult)
            P.c("dve", lambda e: e.scalar_tensor_tensor(out=II, in0=LI, scalar=-1.0, in1=m2, op0=ALU.mult, op1=ALU.mult), [tof(LI), tof(m2)], [tof(II)])
            NPR = [snew() for _ in range(8)]; NPI = [snew() for _ in range(8)]
            P.c("dve", lambda e: e.memset(NPR[0], 1.0), [], [tof(NPR[0])])
            P.c("dve", lambda e: e.memset(NPI[0], 0.0), [], [tof(NPI[0])])
            for k in range(7):
                cmul(NPR[k + 1], NPI[k + 1], NPR[k], NPI[k], IR, II)
            DRk = [snew() for _ in range(8)]; DIk = [snew() for _ in range(8)]
            for k in range(8):
                cmul(DRk[k], DIk[k], PRk[k], PIk[k], CR, CI)
            P.c("dve", lambda e: e.tensor_copy(out=MU[:, 0, 0, :], in_=PRk[8]), [tof(PRk[8])], [Tmu])
            P.c("dve", lambda e: e.tensor_copy(out=MU[:, 0, 1, :], in_=PIk[8]), [tof(PIk[8])], [Tmu])
            for k in range(6):
                t3 = snew(); t4 = snew()
                P.c("dve", lambda e, k=k, t3=t3: e.tensor_tensor(out=t3, in0=MU[:, k, 0, :], in1=MU[:, k, 0, :], op=ALU.mult), [Tmu], [tof(t3)])
                P.c("dve", lambda e, k=k, t4=t4: e.tensor_tensor(out=t4, in0=MU[:, k, 1, :], in1=MU[:, k, 1, :], op=ALU.mult), [Tmu], [tof(t4)])
                P.c("dve", lambda e, k=k: e.scalar_tensor_tensor(out=MU[:, k + 1, 1, :], in0=MU[:, k, 0, :], scalar=2.0, in1=MU[:, k, 1, :],
                                                                op0=ALU.mult, op1=ALU.mult), [Tmu], [Tmu])
                P.c("dve", lambda e, k=k, t3=t3, t4=t4: e.tensor_tensor(out=MU[:, k + 1, 0, :], in0=t3, in1=t4, op=ALU.subtract), [tof(t3), tof(t4), Tmu], [Tmu])
                sidx[0] -= 2
            for k in range(7):
                P.c("dve", lambda e, k=k: e.tensor_scalar(out=MU[:, k, 2, :], in0=MU[:, k, 1, :], scalar1=-1.0, scalar2=None, op0=ALU.mult), [Tmu], [Tmu])
            s_cw[0] = len(S_ops)
            EE, TE = AR.take([2, 16, 8, 16], F32, "EE")
            Ere, Eim = EE[:, 0], EE[:, 1]
            FF3, TF3 = AR.take([2, 16, 8, 16], F32, "FF3")
            F3re, F3im = FF3[:, 0], FF3[:, 1]
            w1, Tw1 = AR.take([16, 16], F32, "w1")
            w2, Tw2 = AR.take([16, 16], F32, "w2")
            w3, Tw3 = AR.take([16, 16], F32, "w3")
            w4, Tw4 = AR.take([16, 16], F32, "w4")
            BRE, BIM, CRE, CIM = Bt[:, 0], Bt[:, 1], Ct[:, 0], Ct[:, 1]
            Fre4 = Fre[:].rearrange("p a (j h) -> p a j h", h=16)
            Fmim4 = Fmim[:].rearrange("p a (j h) -> p a j h", h=16)

            def cw(oR, oI, sR, sI, mR, mI, neg_im, Tout):
                rd = [tof(sR), tof(sI), TBt, TCt]
                P.c("dve", lambda e: e.tensor_tensor(out=w1, in0=mR, in1=bc(sR, 16), op=ALU.mult), rd, [Tw1])
                P.c("dve", lambda e: e.tensor_tensor(out=w2, in0=mI, in1=bc(sI, 16), op=ALU.mult), rd, [Tw2])
                P.c("dve", lambda e: e.tensor_tensor(out=w3, in0=mI, in1=bc(sR, 16), op=ALU.mult), rd, [Tw3])
                P.c("dve", lambda e: e.tensor_tensor(out=w4, in0=mR, in1=bc(sI, 16), op=ALU.mult), rd, [Tw4])
                P.c("dve", lambda e: e.tensor_tensor(out=oR, in0=w1, in1=w2, op=ALU.subtract), [Tw1, Tw2], [Tout])
                if neg_im:
                    P.c("dve", lambda e: e.scalar_tensor_tensor(out=oI, in0=w3, scalar=-1.0, in1=w4, op0=ALU.mult, op1=ALU.subtract), [Tw3, Tw4], [Tout])
                else:
                    P.c("dve", lambda e: e.tensor_tensor(out=oI, in0=w3, in1=w4, op=ALU.add), [Tw3, Tw4], [Tout])
            for i in range(8):
                cw(Ere[:, :, i, :], Eim[:, :, i, :], DRk[7 - i], DIk[7 - i], BRE, BIM, False, TE)
            for j in range(8):
                cw(Fre4[:, :, j, :], Fmim4[:, :, j, :], PRk[j + 1], PIk[j + 1], CRE, CIM, True, Tff)
                cw(F3re[:, :, j, :], F3im[:, :, j, :], NPR[7 - j], NPI[7 - j], CRE, CIM, True, TF3)
            s_tail[0] = len(S_ops)
            for (Esrc, Wdst) in ((Ere, WSre), (Eim, WSim)):
                for pq in range(4):
                    bk, Tbk = nb()

                    def fn(e, Esrc=Esrc, pq=pq, bk=bk):
                        ins = None
                        for pl in range(4):
                            pi_ = pq * 4 + pl
                            ins = e.transpose(bk[:, pl * 128:(pl + 1) * 128], Esrc[:, pi_].rearrange("p i h -> p (i h)"), ident[:])
                        return ins
                    P.c("pe", fn, [TE, Tid], [Tbk])
                    P.c("act", lambda e, bk=bk, Wdst=Wdst, pq=pq: e.activation(out=Wdst[:, pq * 4:(pq + 1) * 4, :].rearrange("p a c -> p (a c)"),
                                                                               in_=bk, func=AF.Identity), [Tbk], [Tws])
            tkt, Ttkt = AR.take([4, 128], F32, "tkt")
            for gq in range(8):
                bk, Tbk = nb()

                for gl in range(4):
                    def fn(e, gq=gq, bk=bk, gl=gl):
                        g = gq * 4 + gl
                        pi_, ee = g // 2, g % 2
                        sl = slice(64 * ee, 64 * ee + 64)
                        e.matmul(bk[:, gl * 128:(gl + 1) * 128], lhsT=Ere[sl, pi_].rearrange("p i h -> p (i h)"),
                                 rhs=F3re[sl, pi_].rearrange("p i h -> p (i h)"), start=True, stop=False)
                        return e.matmul(bk[:, gl * 128:(gl + 1) * 128], lhsT=Eim[sl, pi_].rearrange("p i h -> p (i h)"),
                                        rhs=F3im[sl, pi_].rearrange("p i h -> p (i h)"), start=False, stop=True)
                    P.c("pe", fn, [TE, TF3], [Tbk])
                P.c("dve", lambda e, bk=bk: e.tensor_tensor(out=tkt, in0=bk.rearrange("p (a c) -> p a c", c=128), in1=bcmid(cmask[:], 4), op=ALU.mult),
                    [Tbk, Tcm], [Ttkt])
                P.c("dve", lambda e, gq=gq: e.tensor_tensor(out=TK[:, gq * 4:(gq + 1) * 4, :], in0=tkt, in1=TK[:, gq * 4:(gq + 1) * 4, :], op=ALU.add),
                    [Ttkt, Ttk], [Ttk])
            assert AR.off <= AR.n, "S5 scratch overflow"
        P.barrier()

        def phase_load_x(hf, base=0, ntiles=8, act_only=False):
            AR.reset(base)
            xts = [AR.take([D], F32, "xt%d" % i) for i in range(ntiles)]
            xts = [xts[i % ntiles] for i in range(8)]
            for ttg in range(2):
                for tl in range(4):
                    tix = ttg * 4 + tl
                    xt, Txt = xts[tix]
                    r0 = hf * HALF + tix * 128
                    P.dma(xt, xin[r0:r0 + 128, :], writes=[Txt], q="sp", sem=1)
                for n in range(8):
                    bk, Tbk = nb()

                    def fn(e, n=n, ttg=ttg, bk=bk):
                        ins = None
                        for tl in range(4):
                            xt = xts[ttg * 4 + tl][0]
                            ins = e.transpose(bk[:, tl * 128:(tl + 1) * 128], xt[:, n * 128:(n + 1) * 128], ident[:])
                        return ins
                    P.c("pe", fn, [xts[ttg * 4 + tl][1] for tl in range(4)] + [Tid], [Tbk])
                    eng = "act" if (n % 2 == 0 or act_only) else "dve"
                    if eng == "act":
                        P.c("act", lambda e, n=n, ttg=ttg, bk=bk: e.activation(out=X[:, n, ttg * 512:(ttg + 1) * 512], in_=bk, func=AF.Identity),
                            [Tbk], [TX[n]])
                    else:
                        P.c("dve", lambda e, n=n, ttg=ttg, bk=bk: e.tensor_copy(out=X[:, n, ttg * 512:(ttg + 1) * 512], in_=bk), [Tbk], [TX[n]])
            P.barrier()

        nstate = {}

        def phase_norm(vsrc, voff, shsrc, shoff, final=False, hf=0, base=0, part="all"):
            if part in ("all", "A"):
                AR.reset(base)
                nstate["sqs"] = [AR.take([8, 512], BF16, "sq%d" % i) for i in range(2)]
                nstate["rss"] = [AR.take([512], F32, "rs%d" % i) for i in range(2)]
                nstate["tmps"] = [AR.take([512], F32, "nt%d" % i) for i in range(3)]
            sqs, rss, tmps = nstate["sqs"], nstate["rss"], nstate["tmps"]
            if final:
                ofs = [AR.take([8, 512], F32, "of%d" % i) for i in range(2)]
                ots = [AR.take([D], F32, "ot%d" % i) for i in range(3)]
            oti = 0
            toks = (slice(0, 512), slice(512, 1024))
            for th in (range(2) if part in ("all", "A") else ()):
                sq, Tsq = sqs[th]
                for n in range(8):
                    P.c("act", lambda e, n=n, sq=sq, tok=toks[th]: e.activation(out=sq[:, n, :], in_=X[:, n, tok], func=AF.Square), [TX[n]], [Tsq])
            for th in (range(2) if part in ("all", "A") else ()):
                sq, Tsq = sqs[th]
                rs, Trs = rss[th]
                bk, Tbk = nb()

                def fn(e, sq=sq, bk=bk):
                    ins = None
                    for n in range(8):
                        ins = e.matmul(bk, lhsT=onesm[:], rhs=sq[:, n, :], start=(n == 0), stop=(n == 7))
                    return ins
                P.c("pe", fn, [Tsq, Tones], [Tbk])
                P.c("act", lambda e, rs=rs, bk=bk: e.activation(out=rs, in_=bk, func=AF.Ln, bias=vcol(V_EPS)), [Tbk, Tvec], [Trs])
                P.c("act", lambda e, rs=rs: e.activation(out=rs, in_=rs, func=AF.Exp, scale=-0.5), [Trs], [Trs])
            for th in (range(2) if part in ("all", "B") else ()):
                tok = toks[th]
                rs, Trs = rss[th]
                for n in range(8):
                    if not final:
                        tmp, Ttmp = tmps[n % 3]
                        P.c("dve", lambda e, n=n, tmp=tmp, rs=rs, tok=tok: e.scalar_tensor_tensor(
                            out=tmp, in0=X[:, n, tok], scalar=vsrc[:, voff + n:voff + n + 1], in1=rs, op0=ALU.mult, op1=ALU.mult),
                            [TX[n], Trs, Tcv, Tvec], [Ttmp])
                        P.c("act", lambda e, n=n, tmp=tmp, tok=tok: e.activation(out=U[:, n, tok], in_=tmp, func=AF.Identity,
                                                                                 bias=shsrc[:, shoff + n:shoff + n + 1]), [Ttmp, Tcv], [TU])
                    else:
                        of, Tof = ofs[th]
                        P.c("dve", lambda e, n=n, rs=rs, tok=tok, of=of: e.scalar_tensor_tensor(
                            out=of[:, n, :], in0=X[:, n, tok], scalar=vec[:, V_FG + n:V_FG + n + 1], in1=rs, op0=ALU.mult, op1=ALU.mult),
                            [TX[n], Trs, Tvec], [Tof])
                if final:
                    of, Tof = ofs[th]
                    for tl in range(4):
                        ot, Tot = ots[oti % 3]
                        oti += 1
                        for nh in range(2):
                            bk, Tbk = nb()

                            def fn(e, nh=nh, tl=tl, bk=bk, of=of):
                                ins = None
                                for nl in range(4):
                                    ins = e.transpose(bk[:, nl * 128:(nl + 1) * 128], of[:, nh * 4 + nl, tl * 128:(tl + 1) * 128], ident[:])
                                return ins
                            P.c("pe", fn, [Tof, Tid], [Tbk])
                            if nh == 0:
                                P.c("act", lambda e, ot=ot, bk=bk: e.activation(out=ot[:, 0:512], in_=bk, func=AF.Identity), [Tbk], [Tot])
                            else:
                                P.c("dve", lambda e, ot=ot, bk=bk: e.tensor_copy(out=ot[:, 512:1024], in_=bk), [Tbk], [Tot])
                        r0 = hf * HALF + th * 512 + tl * 128
                        P.dma(out[r0:r0 + 128, :], ot, reads=[Tot], q="sp", sem=2)
            P.barrier()

        ffn_calls = [0]

        def phase_ffn(wup, wdn, goff):
            first = (ffn_calls[0] == 0)
            modg = {0: (8, 20, (2, 3, 4)), 1: (32, 36, (8,))}.get(ffn_calls[0])
            ffn_calls[0] += 1
            AR.reset()
            hid, Th0 = AR.take([NKF, HALF], BF16, "hid")
            Thid = [AR.child(Th0, "hid%d" % k) for k in range(NKF)]
            wus = [AR.take([8, 2, 128], BF16, "wu%d" % i) for i in range(3)]
            wusb = [AR.child(w_[1], "wub") for w_ in wus]
            sls = [AR.take([512], BF16, "sl%d" % i) for i in range(3)]
            if modg:
                mstate["bufs"] = [AR.take([8, 256], BF16, "fmw%d" % i) for i in range(2)]
                mstate["i"] = 0
            assert AR.off <= S_OFF
            AR.reset(S_OFF)
            wd, Tw0 = AR.take([NKF, D], BF16, "wd")
            Twd = [AR.child(Tw0, "wd%d" % k) for k in range(NKF)]
            wup_v = wup.rearrange("(kt kp) c -> kp kt c", kp=128)
            wdn_v = wdn.rearrange("(kt kp) c -> kp kt c", kp=128)
            si = 0
            pend = []
            nextg = [modg[0] if modg else 0]
            endg = modg[1] if modg else 0
            NFL = 16
            n_tail = (s_tail_len[0]) if first else 0
            n_head = (len(S_ops) - n_tail) if first else 0
            head_left = [n_head]
            wdq = list(range(NKF))

            def wu_dma(kk):
                wu_, Twu_ = wus[kk % 3]
                P.dma(wu_[:, :, 0, :], wup_v[:, :, kk * 128:(kk + 1) * 128], writes=[Twu_], q="pool", sem=0)
                P.dma(wu_[:, :, 1, :], wup_v[:, :, DFF + kk * 128:DFF + (kk + 1) * 128], writes=[wusb[kk % 3]], q="pool", sem=0)
            wu_dma(0)
            wu_dma(1)
            for k in range(NKF):
                wu, Twu = wus[k % 3]
                Twub = wusb[k % 3]
                if modg:
                    for ent in pend:
                        mod_mm(ent)
                    pend = []
                    for _ in range(2):
                        if nextg[0] < endg:
                            pend.append(mod_dma(nextg[0]))
                            nextg[0] += 1
                if k + 2 < NKF:
                    wu_dma(k + 2)
                if not first:
                    P.dma(wd[:, k, :], wdn_v[:, k, :], writes=[Twd[k]], q="pool", sem=0)
                elif k >= NFL:
                    assert len(S_ops) == 0
                    for _ in range(4):
                        if wdq:
                            kk = wdq.pop(0)
                            P.dma(wd[:, kk, :], wdn_v[:, kk, :], writes=[Twd[kk]], q="pool", sem=0)
                for th in range(2):
                    tok = slice(th * 512, (th + 1) * 512)
                    bka, Tbka = nb()
                    bkb, Tbkb = nb()

                    def fn(e, wu=wu, bka=bka, bkb=bkb, tok=tok):
                        ins = None
                        for kt in range(8):
                            e.matmul(bka, lhsT=wu[:, kt, 0, :], rhs=U[:, kt, tok], start=(kt == 0), stop=(kt == 7))
                        for kt in range(8):
                            ins = e.matmul(bkb, lhsT=wu[:, kt, 1, :], rhs=U[:, kt, tok], start=(kt == 0), stop=(kt == 7))
                        return ins
                    P.c("pe", fn, [Twu, Twub, TU], [Tbka, Tbkb])
                    sl, Tsl = sls[si % 3]
                    si += 1
                    P.c("act", lambda e, sl=sl, bka=bka: e.activation(out=sl, in_=bka, func=AF.Silu), [Tbka], [Tsl])
                    P.c("dve", lambda e, sl=sl, bkb=bkb, k=k, tok=tok: e.tensor_tensor(out=hid[:, k, tok], in0=sl, in1=bkb, op=ALU.mult),
                        [Tsl, Tbkb], [Thid[k]])
                if first:
                    if k < 11:
                        nf = min((n_head + 10) // 11, head_left[0])
                        head_left[0] -= nf
                        if nf:
                            P.flush(S_ops, nf)
                    elif 13 <= k < NFL:
                        P.flush(S_ops, (n_tail + 2) // 3 if k < NFL - 1 else len(S_ops))
            if modg:
                for ent in pend:
                    mod_mm(ent)
                assert nextg[0] == endg and len(S_ops) == 0
                for m in modg[2]:
                    mod_finish(m)
            for n in range(8):
                bks = [nb() for _ in range(2)]

                def fn(e, n=n, bks=bks):
                    ins = None
                    for kt in range(NKF):
                        for th in range(2):
                            ins = e.matmul(bks[th][0], lhsT=wd[:, kt, n * 128:(n + 1) * 128], rhs=hid[:, kt, th * 512:(th + 1) * 512],
                                           start=(kt == 0), stop=(kt == NKF - 1))
                    return ins
                P.c("pe", fn, Twd + Thid, [bks[0][1], bks[1][1]])
                for th in range(2):
                    tok = slice(th * 512, (th + 1) * 512)
                    bk, Tbk = bks[th]
                    P.c("dve", lambda e, n=n, bk=bk, tok=tok: e.scalar_tensor_tensor(
                        out=X[:, n, tok], in0=bk, scalar=cv[:, goff + n:goff + n + 1], in1=X[:, n, tok], op0=ALU.mult, op1=ALU.add),
                        [Tbk, Tcv, TX[n]], [TX[n]])
            P.barrier()

        win_v = win.rearrange("(kt kp) c -> kp kt c", kp=128)

        def phase_mixer(hf):
            AR.reset()
            ya, Tya = AR.take([8, HALF], BF16, "ya")
            yb, Tyb = AR.take([4, HALF], BF16, "yb")
            base = AR.off
            XT, TXT = AR.take([32, 128], BF16, "XT")
            Hs, THs0 = AR.take([16, 2, 128], BF16, "Hs")
            THs = [AR.child(THs0, "Hs%d" % i) for i in range(16)]
            scratch = AR.off
            Zs = []
            for i in range(4):
                za, Tza = AR.take([2, 192], F32, "za%d" % i)
                zb, Tzb = AR.take([2, 192], F32, "zb%d" % i)
                Zs.append(((za, Tza), (zb, Tzb)))
                P.c("pool", lambda e, za=za: e.memset(za[:, :, 0:64], 0.0), [], [Tza])
                P.c("pool", lambda e, zb=zb: e.memset(zb[:, :, 0:64], 0.0), [], [Tzb])
            wts = [AR.take([8, 128], BF16, "wt%d" % i) for i in range(4)]
            wti = [0]

            def load_wt(col0):
                wt, Twt = wts[wti[0] % 4]
                wti[0] += 1
                P.dma(wt, win_v[:, :, col0:col0 + 128], writes=[Twt], q="pool", sem=0)
                return wt, Twt
            def mkset():
                dct = {}
                dct["xap"] = AR.take([HALF + 8], BF16, "xap")
                dct["xc"] = AR.take([HALF], BF16, "xc")
                dct["ra"] = AR.take([HALF], F32, "ra")
                dct["ig"] = AR.take([HALF], F32, "ig")
                dct["ml"] = AR.take([HALF], F32, "ml")
                dct["u"] = AR.take([HALF], F32, "u")
                dct["gg"] = AR.take([HALF], F32, "gg")
                return dct
            sets = [mkset()]
            set1_off = AR.off
            ths = (slice(0, 512), slice(512, 1024))

            def ks_pairs(pis):
                st_ = []
                for pi_ in pis:
                    (za, Tza), (zb, Tzb) = Zs[pi_ % 4]
                    bk, Tbk = nb()

                    def fn(e, pi_=pi_, bk=bk):
                        ins = None
                        for ee in range(2):
                            sl = slice(64 * ee, 64 * ee + 64)
                            e.matmul(bk[sl, 0:128], lhsT=WSre[:, pi_, sl], rhs=XT[:, 2 * pi_ + ee, :], start=True, stop=True)
                            ins = e.matmul(bk[sl, 128:256], lhsT=WSim[:, pi_, sl], rhs=XT[:, 2 * pi_ + ee, :], start=True, stop=True)
                        return ins
                    P.c("pe", fn, [Tws, TXT], [Tbk])
                    P.c("act", lambda e, za=za, bk=bk: e.activation(out=za[:, :, 64:192], in_=bk[:, 0:256].rearrange("p (a c) -> p a c", c=128),
                                                                    func=AF.Identity), [Tbk], [Tza])
                    st_.append([pi_, (za, Tza), (zb, Tzb)])
                if hf > 0:
                    for step in range(2):
                        for (pi_, (za, Tza), _) in st_:
                            mr = MU[:, 0, 0, pi_:pi_ + 1]; mi = MU[:, 0, 1, pi_:pi_ + 1]; nmi = MU[:, 0, 2, pi_:pi_ + 1]
                            hr = Hin[:, 0, pi_:pi_ + 1]; hi_ = Hin[:, 1, pi_:pi_ + 1]
                            for (dst, src, m) in (((0, hr, mr), (1, hi_, mr)) if step == 0 else ((0, hi_, nmi), (1, hr, mi))):
                                P.c("dve", lambda e, za=za, dst=dst, src=src, m=m: e.scalar_tensor_tensor(
                                    out=za[:, dst, 64:65], in0=src, scalar=m, in1=za[:, dst, 64:65], op0=ALU.mult, op1=ALU.add), [Tza, Thin, Tmu], [Tza])
                for k in range(7):
                    s_ = 1 << k
                    for ent in st_:
                        pi_, (src, Tsrc), (dst, Tdst) = ent
                        mr = MU[:, k, 0, pi_:pi_ + 1]
                        P.c("dve", lambda e, src=src, dst=dst, mr=mr, s_=s_: e.scalar_tensor_tensor(
                            out=dst[:, :, 64:192], in0=src[:, :, 64 - s_:192 - s_], scalar=mr, in1=src[:, :, 64:192],
                            op0=ALU.mult, op1=ALU.add), [Tsrc, Tmu], [Tdst])
                    for (dc, sc2, mix) in ((0, 1, 2), (1, 0, 1)):
                        for ent in st_:
                            pi_, (src, Tsrc), (dst, Tdst) = ent
                            m2 = MU[:, k, mix, pi_:pi_ + 1]
                            P.c("dve", lambda e, src=src, dst=dst, dc=dc, sc2=sc2, m2=m2, s_=s_: e.scalar_tensor_tensor(
                                out=dst[:, dc, 64:192], in0=src[:, sc2, 64 - s_:192 - s_], scalar=m2, in1=dst[:, dc, 64:192],
                                op0=ALU.mult, op1=ALU.add), [Tsrc, Tmu, Tdst], [Tdst])
                    for ent in st_:
                        ent[1], ent[2] = ent[2], ent[1]
                for (pi_, (zf, Tzf), _) in st_:
                    P.c("dve", lambda e, pi_=pi_, zf=zf: e.tensor_copy(out=Hs[:, pi_, :, 1:128], in_=zf[:, :, 64:191]), [Tzf], [THs[pi_]])
                    P.c("dve", lambda e, pi_=pi_: e.tensor_copy(out=Hs[:, pi_, :, 0:1], in_=Hin[:, :, pi_:pi_ + 1]), [Thin], [THs[pi_]])
                    P.c("dve", lambda e, pi_=pi_, zf=zf: e.tensor_copy(out=Hin[:, :, pi_:pi_ + 1], in_=zf[:, :, 191:192]),
                        [Tzf, THs[pi_]], [Thin])

            wtl = {}

            def front(q):
                S_ = sets[q % 2]
                xap, Txap = S_["xap"]; xc, Txc = S_["xc"]; ra, Tra = S_["ra"]; ig, Tig = S_["ig"]
                ml, Tml = S_["ml"]
                wt, Twt = load_wt(q * 128)
                wtl[q] = load_wt(1024 + q * 128)
                P.c("act", lambda e, q=q, xap=xap: e.activation(out=xap[:, 0:4], in_=xatail[:, q, :], func=AF.Identity), [Txat], [Txap])
                for th in range(2):
                    tok = ths[th]
                    bk, Tbk = nb()

                    def fn(e, wt=wt, bk=bk, tok=tok):
                        ins = None
                        for kt in range(8):
                            ins = e.matmul(bk, lhsT=wt[:, kt, :], rhs=U[:, kt, tok], start=(kt == 0), stop=(kt == 7))
                        return ins
                    P.c("pe", fn, [Twt, TU], [Tbk])
                    P.c("act", lambda e, q=q, bk=bk, xap=xap, th=th: e.activation(out=xap[:, 4 + th * 512:4 + (th + 1) * 512], in_=bk, func=AF.Identity,
                                                                                 bias=vcol(V_BIN, q)), [Tbk, Tvec], [Txap])
                P.c("act", lambda e, q=q, xap=xap: e.activation(out=xatail[:, q, :], in_=xap[:, HALF:HALF + 4], func=AF.Identity), [Txap], [Txat])
                gbanks = []
                for th in range(2):
                    tok = ths[th]
                    bk, Tbk = nb()

                    def fn(e, q=q, bk=bk, xap=xap, th=th):
                        ins = None
                        for k in range(4):
                            ins = e.matmul(bk, lhsT=cdiag[:, q, k, :], rhs=xap[:, 1 + th * 512 + k:1 + th * 512 + k + 512], start=(k == 0), stop=(k == 3))
                        return ins
                    P.c("pe", fn, [Tcd, Txap], [Tbk])
                    P.c("act", lambda e, q=q, bk=bk, xc=xc, tok=tok: e.activation(out=xc[:, tok], in_=bk, func=AF.Identity, bias=vcol(V_CONVB, q)),
                        [Tbk, Tvec], [Txc])
                    bkr, Tbkr = nb()
                    bki, Tbki = nb()

                    def fn2(e, q=q, bkr=bkr, bki=bki, xc=xc, tok=tok):
                        e.matmul(bkr, lhsT=WR[:, q, :], rhs=xc[:, tok], start=True, stop=True)
                        return e.matmul(bki, lhsT=WI[:, q, :], rhs=xc[:, tok], start=True, stop=True)
                    P.c("pe", fn2, [Twri, Txc], [Tbkr, Tbki])
                    gbanks.append((bkr, Tbkr, bki, Tbki))
                for th in range(2):
                    tok = ths[th]
                    bkr, Tbkr, bki, Tbki = gbanks[th]
                    P.c("act", lambda e, q=q, bkr=bkr, ra=ra, tok=tok: e.activation(out=ra[:, tok], in_=bkr, func=AF.Sigmoid, bias=vcol(V_LBR, q)),
                        [Tbkr, Tvec], [Tra])
                    P.c("act", lambda e, q=q, bki=bki, ig=ig, tok=tok: e.activation(out=ig[:, tok], in_=bki, func=AF.Sigmoid, bias=vcol(V_LBI, q)),
                        [Tbki, Tvec], [Tig])
                P.c("act", lambda e, q=q, ra=ra, ml=ml: e.activation(out=ml, in_=ra, func=AF.Exp, scale=ccol(C_C2, q)), [Tra, Tcv], [Tml])
                P.c("act", lambda e, q=q, ra=ra: e.activation(out=ra, in_=ra, func=AF.Exp, scale=ccol(C_C1, q)), [Tra, Tcv], [Tra])
                P.c("act", lambda e, ml=ml: e.activation(out=ml, in_=ml, func=AF.Sqrt, scale=-1.0, bias=vcol(V_ONE)), [Tml, Tvec], [Tml])

            def back(q):
                S_ = sets[q % 2]
                xc, Txc = S_["xc"]; ra, Tra = S_["ra"]; ig, Tig = S_["ig"]
                ml, Tml = S_["ml"]; u_, Tu = S_["u"]; gg, Tgg = S_["gg"]
                wtg, Twtg = wtl[q]
                P.c("dve", lambda e, ig=ig, xc=xc, u_=u_: e.tensor_tensor(out=u_, in0=ig, in1=xc, op=ALU.mult), [Tig, Txc], [Tu])
                P.c("dve", lambda e, ml=ml, u_=u_: e.tensor_tensor(out=u_, in0=ml, in1=u_, op=ALU.mult), [Tml, Tu], [Tu])
                P.c("dve", lambda e, q=q, ra=ra, u_=u_: e.tensor_tensor_scan(out=u_, data0=ra, data1=u_, initial=hcar[:, q:q + 1],
                                                                            op0=ALU.mult, op1=ALU.add), [Tra, Tu, Thc], [Tu])
                P.c("dve", lambda e, q=q, u_=u_: e.tensor_copy(out=hcar[:, q:q + 1], in_=u_[:, HALF - 1:HALF]), [Tu], [Thc])
                for th in range(2):
                    tok = ths[th]
                    bk, Tbk = nb()

                    def fn(e, wtg=wtg, bk=bk, tok=tok):
                        ins = None
                        for kt in range(8):
                            ins = e.matmul(bk, lhsT=wtg[:, kt, :], rhs=U[:, kt, tok], start=(kt == 0), stop=(kt == 7))
                        return ins
                    P.c("pe", fn, [Twtg, TU], [Tbk])
                    P.c("act", lambda e, q=q, bk=bk, gg=gg, tok=tok: e.activation(out=gg[:, tok], in_=bk, func=AF.Gelu_apprx_tanh, bias=vcol(V_BIN, 8 + q)),
                        [Tbk, Tvec], [Tgg])

            def back2(q):
                S_ = sets[q % 2]
                u_, Tu = S_["u"]; gg, Tgg = S_["gg"]
                P.c("dve", lambda e, q=q, gg=gg, u_=u_: e.tensor_tensor(out=ya[:, q, :], in0=u_, in1=gg, op=ALU.mult), [Tgg, Tu], [Tya])

            front(0)
            AR.reset(set1_off)
            Xc2, TXc2 = AR.take([32, 8, 16], BF16, "Xc2")
            wxb, Twxb = AR.take([8, 512], BF16, "wxb")
            P.dma(wxb, win_v[:, :, 2048:2560], writes=[Twxb], q="pool", sem=0)
            mq = []
            if hf == 0:
                mstate["bufs"] = [AR.take([8, 256], BF16, "bmw%d" % i) for i in range(3)]
                mstate["i"] = 0
                mnext = [20]
                for _ in range(3):
                    mq.append(mod_dma(mnext[0]))
                    mnext[0] += 1

            def mod_pump():
                if mq:
                    mod_mm(mq.pop(0))
                    if mnext[0] < 32:
                        mq.append(mod_dma(mnext[0]))
                        mnext[0] += 1
            for i in range(8):
                mod_pump()
                bk, Tbk = nb()

                def fn(e, i=i, bk=bk):
                    ins = None
                    for kt in range(8):
                        ins = e.matmul(bk, lhsT=U[:, kt, :].rearrange("p (c i) -> p i c", i=8)[:, i, :], rhs=wxb[:, kt, :],
                                       start=(kt == 0), stop=(kt == 7))
                    return ins
                P.c("pe", fn, [TU, Twxb], [Tbk])
                P.c("dve", lambda e, i=i, bk=bk: e.tensor_tensor(out=Xc2[:, :, i, :], in0=bk.rearrange("p (g h) -> p g h", h=16),
                                                                 in1=bxb[:].rearrange("p (g h) -> p g h", h=16), op=ALU.add), [Tbk, Tbxb], [TXc2])
            for gq in range(8):
                mod_pump()
                bk, Tbk = nb()
                bkb = bk.bitcast(BF16)

                def fn(e, gq=gq, bkb=bkb):
                    ins = None
                    for gl in range(4):
                        ins = e.transpose(bkb[:, gl * 128:(gl + 1) * 128], Xc2[:, gq * 4 + gl].rearrange("p i h -> p (i h)"), identb[:])
                    return ins
                P.c("pe", fn, [TXc2, Tidb], [Tbk])
                P.c("act", lambda e, gq=gq, bkb=bkb: e.activation(out=XT[:, gq * 4:(gq + 1) * 4, :].rearrange("p a c -> p (a c)"),
                                                                  in_=bkb[:, 0:512], func=AF.Identity), [Tbk], [TXT])
            if hf == 0:
                while mq:
                    mod_pump()
                for m in (5, 6, 7):
                    mod_finish(m)
            AR.reset(set1_off)
            sets.append(mkset())
            for q in range(8):
                if q < 7:
                    front(q + 1)
                back(q)
                ks_pairs([2 * q, 2 * q + 1])
                back2(q)
            P.barrier()
            AR.reset(scratch)
            Yg, TYg = AR.take([8, 512], BF16, "Yg")
            ygT, TygT = AR.take([4, HALF], BF16, "ygT")
            gw, Tgw = AR.take([4, 512], BF16, "gw")
            sgs = [AR.take([512], BF16, "sg%d" % i) for i in range(2)]
            P.dma(gw, gluw_d.rearrange("(kt kp) c -> kp kt c", kp=128), writes=[Tgw], q="pool", sem=0)
            W_OFF = 16384
            assert AR.off <= W_OFF
            AR.reset(W_OFF)
            pja, Tpja0 = AR.take([8, D], BF16, "pja"); Tpja = [AR.child(Tpja0, "pja%d" % k) for k in range(8)]
            pjb, Tpjb0 = AR.take([4, D], BF16, "pjb"); Tpjb = [AR.child(Tpjb0, "pjb%d" % k) for k in range(4)]
            wo, Two0 = AR.take([8, D], BF16, "wo"); Two = [AR.child(Two0, "wo%d" % k) for k in range(8)]
            pja_v = pja_d.rearrange("(kt kp) c -> kp kt c", kp=128)
            pjb_v = pjb_d.rearrange("(kt kp) c -> kp kt c", kp=128)
            wo_v = wo_d.rearrange("(kt kp) c -> kp kt c", kp=128)
            for kt in range(8):
                P.dma(pja[:, kt, :], pja_v[:, kt, :], writes=[Tpja[kt]], q="pool", sem=0)
            for kt in range(4):
                P.dma(pjb[:, kt, :], pjb_v[:, kt, :], writes=[Tpjb[kt]], q="pool", sem=0)
            for kt in range(8):
                P.dma(wo[:, kt, :], wo_v[:, kt, :], writes=[Two[kt]], q="pool", sem=0)
            for gq in range(8):
                bk, Tbk = nb()

                def fn(e, gq=gq, bk=bk):
                    ins = None
                    for gl in range(4):
                        g = gq * 4 + gl
                        pi_, ee = g // 2, g % 2
                        sl = slice(64 * ee, 64 * ee + 64)
                        o = bk[:, gl * 128:(gl + 1) * 128]
                        e.matmul(o, lhsT=XT[:, g, :], rhs=TK[:, g, :], start=True, stop=False)
                        e.matmul(o, lhsT=Hs[sl, pi_, 0, :], rhs=Fre[sl, pi_, :], start=False, stop=False)
                        ins = e.matmul(o, lhsT=Hs[sl, pi_, 1, :], rhs=Fmim[sl, pi_, :], start=False, stop=True)
                    return ins
                P.c("pe", fn, [TXT, Ttk, Tff] + THs, [Tbk])
                P.c("act", lambda e, gq=gq, bk=bk: e.activation(
                    out=Yg[:, :, gq * 64:(gq + 1) * 64].rearrange("p j (a h) -> p a j h", a=4),
                    in_=bk.rearrange("p (a j h) -> p a j h", a=4, j=8), func=AF.Gelu_apprx_tanh), [Tbk], [TYg])
            for q in range(4):
                for jh in range(2):
                    bk, Tbk = nb()
                    bkb = bk.bitcast(BF16)

                    def fn(e, q=q, jh=jh, bkb=bkb):
                        ins = None
                        for jl in range(4):
                            ins = e.transpose(bkb[:, jl * 128:(jl + 1) * 128], Yg[:, jh * 4 + jl, q * 128:(q + 1) * 128], identb[:])
                        return ins
                    P.c("pe", fn, [TYg, Tidb], [Tbk])
                    P.c("dve", lambda e, q=q, jh=jh, bkb=bkb: e.tensor_copy(out=ygT[:, q, jh * 512:(jh + 1) * 512], in_=bkb[:, 0:512]),
                        [Tbk], [TygT])
            si = 0
            for q in range(4):
                for th in range(2):
                    tok = ths[th]
                    bk, Tbk = nb()

                    def fn(e, q=q, bk=bk, tok=tok):
                        ins = None
                        for kt in range(4):
                            ins = e.matmul(bk, lhsT=gw[:, kt, q * 128:(q + 1) * 128], rhs=ygT[:, kt, tok], start=(kt == 0), stop=(kt == 3))
                        return ins
                    P.c("pe", fn, [Tgw, TygT], [Tbk])
                    sg, Tsg = sgs[si % 2]
                    si += 1
                    P.c("act", lambda e, q=q, bk=bk, sg=sg: e.activation(out=sg, in_=bk, func=AF.Sigmoid, bias=vcol(V_GLUB, q)), [Tbk, Tvec], [Tsg])
                    P.c("dve", lambda e, q=q, sg=sg, tok=tok: e.tensor_tensor(out=yb[:, q, tok], in0=ygT[:, q, tok], in1=sg, op=ALU.mult),
                        [Tsg, TygT], [Tyb])
            P.barrier()
            AR.reset(base)
            m_, Tm = AR.take([8, HALF], BF16, "m")
            wts = [AR.take([8, 128], BF16, "mwt%d" % i) for i in range(4)]
            sas = [AR.take([512], F32, "sa%d" % i) for i in range(2)]
            sbs = [AR.take([512], F32, "sb%d" % i) for i in range(2)]
            t1s = [AR.take([512], F32, "t1%d" % i) for i in range(2)]
            t2s = [AR.take([512], F32, "t2%d" % i) for i in range(2)]
            assert AR.off <= W_OFF
            ci = 0
            for n in range(8):
                wta, Twta = load_wt(2560 + n * 128)
                wtb, Twtb = load_wt(3584 + n * 128)
                for th in range(2):
                    tok = slice(th * 512, (th + 1) * 512)
                    bka, Tbka = nb(); bkb, Tbkb = nb(); bkc, Tbkc = nb(); bkd, Tbkd = nb()

                    def fn(e, n=n, bka=bka, bkb=bkb, tok=tok, th=th):
                        ins = None
                        for kt in range(8):
                            e.matmul(bka, lhsT=pja[:, kt, n * 128:(n + 1) * 128], rhs=ya[:, kt, tok], start=(kt == 0), stop=(kt == 7))
                        for kt in range(4):
                            ins = e.matmul(bkb, lhsT=pjb[:, kt, n * 128:(n + 1) * 128],
                                           rhs=yb[:, kt, :].rearrange("p (j c) -> p j c", j=8)[:, :, th * 64:(th + 1) * 64],
                                           start=(kt == 0), stop=(kt == 3))
                        return ins
                    P.c("pe", fn, Tpja + Tpjb + [Tya, Tyb], [Tbka, Tbkb])

                    def fn2(e, wta=wta, wtb=wtb, bkc=bkc, bkd=bkd, tok=tok):
                        ins = None
                        for kt in range(8):
                            e.matmul(bkc, lhsT=wta[:, kt, :], rhs=U[:, kt, tok], start=(kt == 0), stop=(kt == 7))
                        for kt in range(8):
                            ins = e.matmul(bkd, lhsT=wtb[:, kt, :], rhs=U[:, kt, tok], start=(kt == 0), stop=(kt == 7))
                        return ins
                    P.c("pe", fn2, [Twta, Twtb, TU], [Tbkc, Tbkd])
                    sa, Tsa = sas[ci % 2]; sb_, Tsb = sbs[ci % 2]; t1_, Tt1 = t1s[ci % 2]; t2_, Tt2 = t2s[ci % 2]
                    ci += 1
                    P.c("act", lambda e, n=n, bkc=bkc, sa=sa: e.activation(out=sa, in_=bkc, func=AF.Sigmoid, bias=vcol(V_BIN, 20 + n)), [Tbkc, Tvec], [Tsa])
                    P.c("act", lambda e, n=n, bkd=bkd, sb_=sb_: e.activation(out=sb_, in_=bkd, func=AF.Sigmoid, bias=vcol(V_BIN, 28 + n)), [Tbkd, Tvec], [Tsb])
                    P.c("dve", lambda e, sa=sa, bka=bka, t1_=t1_: e.tensor_tensor(out=t1_, in0=sa, in1=bka, op=ALU.mult), [Tsa, Tbka], [Tt1])
                    P.c("dve", lambda e, sb_=sb_, bkb=bkb, t2_=t2_: e.tensor_tensor(
                        out=t2_.rearrange("p (c j) -> p c j", j=8), in0=sb_.rearrange("p (c j) -> p c j", j=8),
                        in1=bkb.rearrange("p (j c) -> p c j", j=8), op=ALU.mult), [Tsb, Tbkb], [Tt2])
                    P.c("dve", lambda e, n=n, t1_=t1_, t2_=t2_, tok=tok: e.tensor_tensor(out=m_[:, n, tok], in0=t1_, in1=t2_, op=ALU.add), [Tt1, Tt2], [Tm])
            for n in range(8):
                for th in range(2):
                    tok = slice(th * 512, (th + 1) * 512)
                    bk, Tbk = nb()

                    def fn(e, n=n, bk=bk, tok=tok):
                        ins = None
                        for kt in range(8):
                            ins = e.matmul(bk, lhsT=wo[:, kt, n * 128:(n + 1) * 128], rhs=m_[:, kt, tok], start=(kt == 0), stop=(kt == 7))
                        return ins
                    P.c("pe", fn, Two + [Tm], [Tbk])
                    P.c("dve", lambda e, n=n, bk=bk, tok=tok: e.scalar_tensor_tensor(
                        out=X[:, n, tok], in0=bk, scalar=cv[:, 40 + n:40 + n + 1], in1=X[:, n, tok], op0=ALU.mult, op1=ALU.add),
                        [Tbk, Tcv, TX[n]], [TX[n]])
            P.barrier()

        sched = []
        P.deferq = S_ops
        phase_S()
        P.deferq = None
        n_tail = len(S_ops) - s_tail[0]
        s_tail_len[0] = n_tail
        P.flush(S_ops, 60)
        mod_first()
        phase_load_x(0, base=3200, ntiles=4, act_only=True)
        phase_norm(cv, C_V1, cv, 0, base=4608, part="A")
        P.flush(S_ops, min(110, s_cw[0] - 60))
        mod_finish(0)
        mod_finish(1)
        phase_norm(cv, C_V1, cv, 0, part="B")
        for hf in range(2):
            if hf > 0:
                sched.append(lambda hf=hf: phase_load_x(hf))
                sched.append(lambda: phase_norm(cv, C_V1, cv, 0))
            sched.append(lambda: phase_ffn(wup1, wdn1, C_G1H))
            sched.append(lambda: phase_norm(cv, C_V2, cv, 24))
            sched.append(lambda hf=hf: phase_mixer(hf))
            sched.append(lambda: phase_norm(cv, C_V3, cv, 48))
            sched.append(lambda: phase_ffn(wup2, wdn2, C_G3H))
            sched.append(lambda hf=hf: phase_norm(None, 0, None, 0, final=True, hf=hf))
        for f in sched[:STOP]:
            f()
        P.emit()
    return nc


_NC_CACHE = {}


def _prep_shared(inp):
    f = np.float32
    sh = {}
    sh["modw"] = np.ascontiguousarray(inp["mod_w"][0], dtype=f)
    sh["wup1"] = np.ascontiguousarray(inp["ffn1_w_up"][0], dtype=f)
    sh["wdn1"] = np.ascontiguousarray(inp["ffn1_w_down"][0], dtype=f)
    sh["wup2"] = np.ascontiguousarray(inp["ffn2_w_up"][0], dtype=f)
    sh["wdn2"] = np.ascontiguousarray(inp["ffn2_w_down"][0], dtype=f)
    sh["win"] = np.ascontiguousarray(inp["w_in"][0], dtype=f)
    sh["pja"] = np.ascontiguousarray(inp["proj_a"][0], dtype=f)
    sh["pjb"] = np.ascontiguousarray(inp["proj_b"][0], dtype=f)
    sh["wo"] = np.ascontiguousarray(inp["w_out"][0], dtype=f)
    sh["gluw"] = np.ascontiguousarray(inp["glu_w"][0], dtype=f)

    def pt(v):
        v = np.asarray(v, dtype=f)
        return np.ascontiguousarray(v.reshape(-1, 128).T)
    vecs = np.zeros((128, NV), f)
    vecs[:, V_MODB:V_MODB + 72] = pt(inp["mod_b"][0])
    vecs[:, V_N1G:V_N1G + 8] = pt(inp["norm1_g"][0])
    vecs[:, V_N2G:V_N2G + 8] = pt(inp["norm2_g"][0])
    vecs[:, V_N3G:V_N3G + 8] = pt(inp["norm3_g"][0])
    vecs[:, V_FG:V_FG + 8] = pt(inp["final_g"])
    vecs[:, V_BIN:V_BIN + 36] = pt(inp["b_in"][0])
    cw = np.asarray(inp["conv_w"][0], dtype=f)
    vecs[:, V_CONVW:V_CONVW + 32] = cw.reshape(4, 8, 128).transpose(2, 1, 0).reshape(128, 32)
    vecs[:, V_CONVB:V_CONVB + 8] = pt(inp["conv_b"][0])
    vecs[:, V_LBR:V_LBR + 8] = pt(inp["lru_b_r"][0])
    vecs[:, V_LBI:V_LBI + 8] = pt(inp["lru_b_i"][0])
    vecs[:, V_LAM:V_LAM + 8] = pt(inp["lru_lambda"][0])
    vecs[:, V_GLUB:V_GLUB + 4] = pt(inp["glu_b"][0])
    vecs[:, V_EPS] = EPS
    vecs[:, V_ONE] = 1.0
    sh["vecs"] = vecs
    for name, key in (("wrbd", "lru_w_r"), ("wibd", "lru_w_i")):
        w = np.asarray(inp[key][0], dtype=f)
        bd = np.zeros((128, 8, 128), f)
        for h in range(16):
            q, e = h // 2, h % 2
            bd[64 * e:64 * e + 64, q, 64 * e:64 * e + 64] = w[h]
        sh[name] = bd
    def pair(a):
        a = np.asarray(a, dtype=f)
        a = a.reshape((16, 2, 64) + a.shape[2:])
        perm = (1, 2, 0) + tuple(range(3, a.ndim))
        a = a.transpose(perm)
        return np.ascontiguousarray(a.reshape((128, 16) + a.shape[3:]))
    s5sc = np.zeros((128, 3, 16), f)
    s5sc[:, 0] = pair(inp["s5_a_re"][0])
    s5sc[:, 1] = pair(inp["s5_a_im"][0])
    s5sc[:, 2] = pair(np.repeat(np.asarray(inp["s5_log_dt"][0], dtype=f)[:, None], 64, axis=1))
    sh["s5sc"] = s5sc
    s5b = np.zeros((128, 2, 16, 16), f)
    s5b[:, 0] = pair(inp["s5_b_re"][0])
    s5b[:, 1] = pair(inp["s5_b_im"][0])
    sh["s5b"] = s5b
    s5c = np.zeros((128, 2, 16, 16), f)
    s5c[:, 0] = pair(np.asarray(inp["s5_c_re"][0]).transpose(0, 2, 1))
    s5c[:, 1] = pair(np.asarray(inp["s5_c_im"][0]).transpose(0, 2, 1))
    sh["s5c"] = s5c
    dm = np.zeros((8, 16, 32, 8, 16), f)
    d = np.asarray(inp["s5_d"][0], dtype=f).reshape(32, 16)
    for i in range(8):
        for h in range(16):
            dm[i, h, :, i, h] = d[:, h]
    sh["dmat"] = dm.reshape(128, 32, 128)
    cm = np.zeros((8, 16, 8, 16), f)
    for i in range(8):
        cm[i, :, i:, :] = 1.0
    sh["cmask"] = cm.reshape(128, 128)
    bxb = np.asarray(inp["b_in"][0], dtype=f)[2048:2560]
    sh["bxb"] = np.ascontiguousarray(np.broadcast_to(bxb[None, :], (128, 512)))
    sh["ident"] = np.eye(128, dtype=f)
    sh["onesm"] = np.full((128, 128), 1.0 / 1024.0, f)
    return sh


def kernel(**inputs):
    if "nc" not in _NC_CACHE:
        _NC_CACHE["nc"] = build_nc()
    nc = _NC_CACHE["nc"]
    sh = _prep_shared(inputs)
    x = np.asarray(inputs["x"], dtype=np.float32)
    c = np.asarray(inputs["c"], dtype=np.float32)
    in_maps = []
    for b in range(8):
        m = dict(sh)
        m["xin"] = np.ascontiguousarray(x[b])
        m["cvec"] = np.ascontiguousarray(c[b].reshape(8, 128).T)
        in_maps.append(m)
    res = run_bass_kernel_spmd(nc, in_maps, core_ids=list(range(8)))
    return np.stack([np.asarray(r["out"], dtype=np.float32) for r in res.results], axis=0)
```

```python
import contextlib
import math
import numpy as np
import concourse.bass as bass
import concourse.mybir as mybir
from concourse.bass_utils import run_bass_kernel_spmd

F32 = mybir.dt.float32
BF16 = mybir.dt.bfloat16
AF = mybir.ActivationFunctionType
ALU = mybir.AluOpType

ENGS = ("pe", "dve", "act", "pool", "sp")
D = 1024
SEQ = 2048
HALF = 1024
DFF = 2816
NKF = 22
EPS = 1e-6
STOP = 100
USE_BARRIERS = False
SCUT = 100

V_MODB, V_N1G, V_N2G, V_N3G, V_FG, V_BIN, V_CONVW, V_CONVB, V_LBR, V_LBI, V_LAM, V_GLUB, V_EPS, V_ONE, V_NPI = (
    0, 72, 80, 88, 96, 104, 140, 172, 180, 188, 196, 204, 208, 209, 210)
NV = 212
C_MOD, C_V1, C_V2, C_V3, C_G1H, C_G3H, C_C1, C_TMP, C_C2 = 0, 72, 80, 88, 96, 104, 112, 120, 128
NCV = 160


class T:
    __slots__ = ("w", "rs", "name", "parents", "entry")

    def __init__(self, name=""):
        self.w = None
        self.rs = []
        self.name = name
        self.parents = []
        self.entry = None


class Op:
    __slots__ = ("eng", "fn", "deps", "kind", "sig", "cnt", "dsem", "dcnt", "waits")

    def __init__(self, eng, fn, kind):
        self.eng = eng
        self.fn = fn
        self.kind = kind
        self.deps = []
        self.sig = False
        self.cnt = 0
        self.dsem = None
        self.dcnt = 0
        self.waits = []


class Prog:
    RINGS = {0: (0, 16), 1: (16, 8), 2: (24, 8)}

    def __init__(self, nc, n_dma_sems=32):
        self.nc = nc
        self.ops = {e: [] for e in ENGS}
        self.n_dma_sems = n_dma_sems
        self.dma_counts = [0] * n_dma_sems
        self.last_dma = [None] * n_dma_sems
        self.ring_pos = {k: 0 for k in self.RINGS}
        self.bar = {e: [] for e in ENGS}
        self.deferq = None
        self.atomic_ts = set()

    def barrier(self, force=False):
        if not (USE_BARRIERS or force):
            return
        deps = []
        for e in ENGS:
            for op in reversed(self.ops[e]):
                if op.kind == "c":
                    deps.append(op)
                    break
        for d in self.last_dma:
            if d is not None:
                deps.append(d)
        for e in ENGS:
            self.bar[e] = list(deps)

    def _add(self, eng, fn, reads, writes, kind="c", dsem=None):
        op = Op(eng, fn, kind)
        deps = list(self.bar[eng])
        self.bar[eng] = []
        for t in reads:
            if t.w is not None:
                deps.append(t.w)
            for p in t.parents:
                if p.w is not None:
                    deps.append(p.w)
                deps.extend(p.rs)
        for t in writes:
            if t.w is not None:
                deps.append(t.w)
            deps.extend(t.rs)
            for p in t.parents:
                if p.w is not None:
                    deps.append(p.w)
                deps.extend(p.rs)
            t.parents = []
        seen = set()
        for d in deps:
            if id(d) not in seen:
                seen.add(id(d))
                op.deps.append(d)
        for t in reads:
            t.rs.append(op)
        for t in writes:
            t.w = op
            t.rs = []
        self.ops[eng].append(op)
        if kind == "d":
            base, size = self.RINGS[dsem]
            si = base + self.ring_pos[dsem] % size
            self.ring_pos[dsem] += 1
            prev = self.last_dma[si]
            if prev is not None and prev not in op.deps:
                op.deps.append(prev)
            op.dsem = si
            self.dma_counts[si] += 16
            op.dcnt = self.dma_counts[si]
            self.last_dma[si] = op
        return op

    def c(self, eng, fn, reads=(), writes=()):
        if self.deferq is not None:
            self.deferq.append((eng, fn, list(reads), list(writes), "c", None))
            return None
        return self._add(eng, fn, list(reads), list(writes))

    def dma(self, out_ap, in_ap, reads=(), writes=(), q="sp", sem=0):
        def fn(e, out_ap=out_ap, in_ap=in_ap):
            return e.dma_start(out=out_ap, in_=in_ap)
        if self.deferq is not None:
            self.deferq.append((q, fn, list(reads), list(writes), "d", sem))
            return None
        return self._add(q, fn, list(reads), list(writes), kind="d", dsem=sem)

    def flush(self, q, n):
        cnt = 0
        open_ = set()
        while q and (cnt < n or open_):
            eng, fn, rd, wr, kind, sem = q.pop(0)
            self._add(eng, fn, rd, wr, kind=kind, dsem=sem)
            cnt += 1
            for t in rd:
                open_.discard(id(t))
            for t in wr:
                if id(t) in self.atomic_ts:
                    open_.add(id(t))

    def emit(self):
        nc = self.nc
        for e in ENGS:
            for op in self.ops[e]:
                for d in op.deps:
                    if d.kind == "c":
                        d.sig = True
        for e in ENGS:
            n = 0
            for op in self.ops[e]:
                if op.kind == "c" and op.sig:
                    n += 1
                    op.cnt = n
        for e in ENGS:
            seen = {}
            for op in self.ops[e]:
                need = {}
                for d in op.deps:
                    if d.kind == "c":
                        key = ("c", d.eng)
                        val = d.cnt
                    else:
                        key = ("d", d.dsem)
                        val = d.dcnt
                    if val > need.get(key, 0):
                        need[key] = val
                for key, val in need.items():
                    if seen.get(key, 0) >= val:
                        continue
                    seen[key] = val
                    op.waits.append((key, val))
        with contextlib.ExitStack() as st:
            csem = {e: st.enter_context(nc.semaphore("c_" + e)) for e in ENGS}
            dsem = [st.enter_context(nc.semaphore("d_%d" % i)) for i in range(self.n_dma_sems)]
            block = st.enter_context(nc.Block())

            def run(eng_name):
                def body(e):
                    for op in self.ops[eng_name]:
                        for key, val in op.waits:
                            s = csem[key[1]] if key[0] == "c" else dsem[key[1]]
                            e.wait_ge(s, val)
                        ins = op.fn(e)
                        if op.kind == "d":
                            ins.then_inc(dsem[op.dsem], 16)
                        elif op.sig:
                            ins.then_inc(csem[eng_name], 1)
                    if eng_name == "sp":
                        for i in range(self.n_dma_sems):
                            if self.dma_counts[i] > 0:
                                e.wait_ge(dsem[i], self.dma_counts[i])
                return body

            block.tensor(run("pe"))
            block.vector(run("dve"))
            block.scalar(run("act"))
            block.gpsimd(run("pool"))
            block.sync(run("sp"))


class Arena:
    def __init__(self, ap):
        self.ap = ap
        self.n = ap.shape[1]
        self.off = 0
        self.takes = []

    def reset(self, off=0):
        self.off = off

    def take(self, free_shape, dt, name=""):
        nel = int(np.prod(free_shape))
        nwords = nel if dt == F32 else (nel + 1) // 2
        nwords = (nwords + 7) // 8 * 8
        assert self.off + nwords <= self.n, ("arena overflow", name, self.off, nwords, self.n)
        v = self.ap[:, self.off:self.off + nwords]
        start, end = self.off, self.off + nwords
        self.off += nwords
        parents = []
        keep = []
        for ent in self.takes:
            if ent[0] < end and start < ent[1]:
                parents.extend(ent[2])
                if start <= ent[0] and ent[1] <= end:
                    continue
            keep.append(ent)
        self.takes = keep
        if dt != F32:
            v = v.bitcast(dt)
        v = v[:, 0:nel]
        if len(free_shape) == 2:
            v = v.rearrange("p (a b) -> p a b", b=free_shape[1])
        elif len(free_shape) == 3:
            v = v.rearrange("p (a b c) -> p a b c", b=free_shape[1], c=free_shape[2])
        elif len(free_shape) == 4:
            v = v.rearrange("p (a b c d) -> p a b c d", b=free_shape[1], c=free_shape[2], d=free_shape[3])
        t = T(name)
        t.parents = list(parents)
        t.entry = [start, end, [t], parents]
        self.takes.append(t.entry)
        return v, t

    def child(self, t, name=""):
        c = T(name)
        c.parents = list(t.entry[3])
        c.entry = t.entry
        t.entry[2].append(c)
        return c


def bc(ap, n):
    return bass.AP(ap.tensor, ap.offset, [list(x) for x in ap.ap] + [[0, n]])


def bcmid(ap, n):
    l = [list(x) for x in ap.ap]
    return bass.AP(ap.tensor, ap.offset, [l[0], [0, n]] + l[1:])


def build_nc():
    nc = bass.Bass("TRN2", target_bir_lowering=False)
    dr = {}

    def din(name, shape):
        dr[name] = nc.dram_tensor(name, list(shape), F32, kind="ExternalInput").ap()
        return dr[name]

    xin = din("xin", [SEQ, D])
    cvec = din("cvec", [128, 8])
    modw = din("modw", [D, 9 * D])
    vecs = din("vecs", [128, NV])
    wup1 = din("wup1", [D, 2 * DFF])
    wdn1 = din("wdn1", [DFF, D])
    wup2 = din("wup2", [D, 2 * DFF])
    wdn2 = din("wdn2", [DFF, D])
    win = din("win", [D, 4608])
    wrbd = din("wrbd", [128, 8, 128])
    wibd = din("wibd", [128, 8, 128])
    pja_d = din("pja", [D, D])
    pjb_d = din("pjb", [512, D])
    wo_d = din("wo", [D, D])
    gluw_d = din("gluw", [512, 512])
    s5sc_d = din("s5sc", [128, 3, 16])
    s5b_d = din("s5b", [128, 2, 16, 16])
    s5c_d = din("s5c", [128, 2, 16, 16])
    dmat_d = din("dmat", [128, 32, 128])
    cmask_d = din("cmask", [128, 128])
    bxb_d = din("bxb", [128, 512])
    ident_d = din("ident", [128, 128])
    onesm_d = din("onesm", [128, 128])
    out = nc.dram_tensor("out", [SEQ, D], F32, kind="ExternalOutput").ap()

    with contextlib.ExitStack() as st:
        def sb(name, shape, dt):
            return st.enter_context(nc.sbuf_tensor(name, shape, dt))

        P = Prog(nc)
        X = sb("X", [128, 8, HALF], F32); TX = [T("X%d" % n) for n in range(8)]
        U = sb("U", [128, 8, HALF], BF16); TU = T("U")
        vec = sb("vec", [128, NV], F32); Tvec = T("vec")
        cv = sb("cv", [128, NCV], F32); Tcv = T("cv")
        ident = sb("identf", [128, 128], F32); Tid = T("id")
        identb = sb("identb", [128, 128], BF16); Tidb = T("idb")
        onesm = sb("onesm_s", [128, 128], BF16); Tones = T("ones")
        cmask = sb("cmask_s", [128, 128], F32); Tcm = T("cmask")
        bxb = sb("bxb_s", [128, 512], F32); Tbxb = T("bxb")
        cdiag = sb("cdiag", [128, 8, 4, 128], BF16); Tcd = T("cdiag")
        WR = sb("WR", [128, 8, 128], BF16); WI = sb("WI", [128, 8, 128], BF16); Twri = T("wri")
        WSre = sb("WSre", [128, 16, 128], BF16); WSim = sb("WSim", [128, 16, 128], BF16); Tws = T("ws")
        TK = sb("TK", [128, 32, 128], BF16); Ttk = T("tk")
        Fre = sb("Fre", [128, 16, 128], BF16); Fmim = sb("Fmim", [128, 16, 128], BF16); Tff = T("ff")
        MU = sb("MU", [128, 7, 3, 16], F32); Tmu = T("mu")
        xatail = sb("xatail", [128, 8, 4], BF16); Txat = T("xatail")
        hcar = sb("hcar", [128, 8], F32); Thc = T("hcar")
        Hin = sb("Hin", [128, 2, 16], F32); Thin = T("hin")
        cactb = sb("cactb", [128, 8], BF16); Tcact = T("cact")
        arena_t = sb("arena", [128, 29952], F32)
        AR = Arena(arena_t[:])

        banks = []
        for i in range(8):
            pt = st.enter_context(nc.psum_tensor("bank%d" % i, [128, 512], F32))
            banks.append((pt[:], T("bank%d" % i)))
        bank_i = [0]
        P.atomic_ts = set(id(t) for (_, t) in banks)

        bank_skip = set()

        def nb():
            while (bank_i[0] % 8) in bank_skip:
                bank_i[0] += 1
            b = banks[bank_i[0] % 8]
            bank_i[0] += 1
            return b

        def vcol(off, n=None):
            return vec[:, off:off + 1] if n is None else vec[:, off + n:off + n + 1]

        def ccol(off, n):
            return cv[:, off + n:off + n + 1]

        P.dma(vec[:], vecs, writes=[Tvec], q="sp", sem=1)
        P.dma(ident[:], ident_d, writes=[Tid], q="sp", sem=1)
        P.dma(cmask[:], cmask_d, writes=[Tcm], q="sp", sem=1)
        P.dma(bxb[:], bxb_d, writes=[Tbxb], q="sp", sem=1)
        P.dma(identb[:], ident_d, writes=[Tidb], q="pool", sem=0)
        P.dma(onesm[:], onesm_d, writes=[Tones], q="pool", sem=0)
        P.dma(WR[:], wrbd, writes=[Twri], q="pool", sem=0)
        P.dma(WI[:], wibd, writes=[Twri], q="pool", sem=0)
        P.c("dve", lambda e: e.memset(xatail[:], 0.0), [], [Txat])
        P.c("dve", lambda e: e.memset(hcar[:], 0.0), [], [Thc])
        P.c("dve", lambda e: e.memset(Hin[:], 0.0), [], [Thin])

        AR.reset()
        cin, Tcin = AR.take([8], F32, "cin")
        P.dma(cin, cvec, writes=[Tcin], q="sp", sem=1)
        P.c("act", lambda e: e.activation(out=cactb[:], in_=cin, func=AF.Silu), [Tcin], [Tcact])
        mbufs0 = [AR.take([8, 256], BF16, "mw%d" % i) for i in range(3)]
        pmod, Tpmod = nb()
        bank_skip.add((bank_i[0] - 1) % 8)
        modw_v = modw.rearrange("(kt kp) c -> kp kt c", kp=128)
        mstate = {"bufs": mbufs0, "i": 0}

        def mod_dma(g):
            mb, Tmb = mstate["bufs"][mstate["i"] % len(mstate["bufs"])]
            mstate["i"] += 1
            P.dma(mb, modw_v[:, :, g * 256:(g + 1) * 256], writes=[Tmb], q="pool", sem=0)
            return (g, mb, Tmb)

        def mod_mm(ent):
            g, mb, Tmb = ent

            def fn(e, mb=mb, g=g):
                ins = None
                for tl in range(2):
                    t = g * 2 + tl
                    for kt in range(8):
                        ins = e.matmul(pmod[:, t:t + 1], lhsT=mb[:, kt, tl * 128:(tl + 1) * 128],
                                       rhs=cactb[:, kt:kt + 1], start=(kt == 0), stop=(kt == 7))
                return ins
            P.c("pe", fn, [Tmb, Tcact], [Tpmod])

        def mod_finish(m):
            P.c("dve", lambda e, m=m: e.tensor_tensor(out=cv[:, 8 * m:8 * m + 8], in0=pmod[:, 8 * m:8 * m + 8],
                                                      in1=vec[:, V_MODB + 8 * m:V_MODB + 8 * m + 8], op=ALU.add), [Tpmod, Tvec], [Tcv])
            der = {1: (C_V1, V_N1G), 4: (C_V2, V_N2G), 7: (C_V3, V_N3G)}
            if m in der:
                coff, goff = der[m]
                P.c("dve", lambda e, coff=coff, m=m: e.tensor_scalar(out=cv[:, coff:coff + 8], in0=cv[:, 8 * m:8 * m + 8],
                                                                      scalar1=1.0, scalar2=None, op0=ALU.add), [Tcv], [Tcv])
                P.c("dve", lambda e, coff=coff, goff=goff: e.tensor_tensor(out=cv[:, coff:coff + 8], in0=cv[:, coff:coff + 8],
                                                                            in1=vec[:, goff:goff + 8], op=ALU.mult), [Tcv, Tvec], [Tcv])
            if m == 2:
                P.c("dve", lambda e: e.tensor_scalar(out=cv[:, C_G1H:C_G1H + 8], in0=cv[:, 16:24], scalar1=0.5, scalar2=None, op0=ALU.mult), [Tcv], [Tcv])
            if m == 8:
                P.c("dve", lambda e: e.tensor_scalar(out=cv[:, C_G3H:C_G3H + 8], in0=cv[:, 64:72], scalar1=0.5, scalar2=None, op0=ALU.mult), [Tcv], [Tcv])
                bank_skip.clear()

        ents = [mod_dma(g) for g in range(3)]

        def mod_first():
            for g in range(8):
                mod_mm(ents[g])
                if g + 3 < 8:
                    ents.append(mod_dma(g + 3))
        S_ops = []
        s_tail_start = 0
        s_tail = [0]
        s_cw = [0]
        s_tail_len = [0]
        S_OFF = 17152

        def startup_rest():
            for g in range(8, 36):
                mod_mm(ents[g])
                if g + 3 < 36:
                    ents.append(mod_dma(g + 3))
            for m in range(2, 9):
                mod_finish(m)
        def phase_S():
            P.c("act", lambda e: e.activation(out=cv[:, C_TMP:C_TMP + 8], in_=vec[:, V_LAM:V_LAM + 8], func=AF.Exp, scale=-1.0), [Tvec], [Tcv])
            P.c("act", lambda e: e.activation(out=cv[:, C_TMP:C_TMP + 8], in_=cv[:, C_TMP:C_TMP + 8], func=AF.Ln, bias=vcol(V_ONE)), [Tcv, Tvec], [Tcv])
            P.c("dve", lambda e: e.tensor_scalar(out=cv[:, C_C1:C_C1 + 8], in0=cv[:, C_TMP:C_TMP + 8], scalar1=-8.0, scalar2=None, op0=ALU.mult), [Tcv], [Tcv])
            P.c("dve", lambda e: e.tensor_scalar(out=cv[:, C_C2:C_C2 + 8], in0=cv[:, C_TMP:C_TMP + 8], scalar1=-16.0, scalar2=None, op0=ALU.mult), [Tcv], [Tcv])
            for q in range(8):
                for k in range(4):
                    P.c("dve", lambda e, q=q, k=k: e.tensor_scalar(out=cdiag[:, q, k, :], in0=identb[:], scalar1=vcol(V_CONVW, q * 4 + k),
                                                                   scalar2=None, op0=ALU.mult), [Tidb, Tvec], [Tcd])
            AR.reset(S_OFF)
            NS = 104
            scs, Tsc = AR.take([NS, 16], F32, "scs")
            sidx = [0]

            def snew():
                i = sidx[0]
                sidx[0] += 1
                assert i < NS
                return scs[:, i, :]
            s5in, Ts5in = AR.take([3, 16], F32, "s5in")
            Bt, TBt = AR.take([2, 16, 16], F32, "Bt")
            Ct, TCt = AR.take([2, 16, 16], F32, "Ct")
            P.dma(s5in, s5sc_d, writes=[Ts5in], q="sp", sem=1)
            P.dma(Bt, s5b_d, writes=[TBt], q="sp", sem=1)
            P.dma(Ct, s5c_d, writes=[TCt], q="sp", sem=1)
            P.dma(TK[:], dmat_d, writes=[Ttk], q="pool", sem=0)
            TSD = {}

            def tof(ap):
                key = (ap.tensor.name, int(ap.offset))
                if key not in TSD:
                    if ap.tensor.name == "arena":
                        off = int(ap.offset)
                        ent = [en for en in AR.takes if en[0] <= off < en[1]]
                        assert len(ent) == 1
                        TSD[key] = AR.child(ent[0][2][0], "s%d" % len(TSD))
                    else:
                        TSD[key] = T("s%d" % len(TSD))
                return TSD[key]

            def tt(o, a, b, op):
                P.c("dve", lambda e: e.tensor_tensor(out=o, in0=a, in1=b, op=op), [tof(a), tof(b), Ts5in], [tof(o)])

            def ts(o, a, s1, op0, s2=None, op1=None):
                if op1 is None:
                    P.c("dve", lambda e: e.tensor_scalar(out=o, in0=a, scalar1=s1, scalar2=None, op0=op0), [tof(a), Ts5in], [tof(o)])
                else:
                    P.c("dve", lambda e: e.tensor_scalar(out=o, in0=a, scalar1=s1, scalar2=s2, op0=op0, op1=op1), [tof(a), Ts5in], [tof(o)])

            def act(o, a, func, scale=1.0):
                P.c("act", lambda e: e.activation(out=o, in_=a, func=func, scale=scale), [tof(a), Ts5in], [tof(o)])
            tq = [snew() for _ in range(8)]
            cmi = [0]

            def cmul(oR, oI, aR, aI, bR, bI):
                ta, tb, tc_, td_ = tq[4 * (cmi[0] % 2):4 * (cmi[0] % 2) + 4]
                cmi[0] += 1
                tt(ta, aR, bR, ALU.mult)
                tt(tb, aI, bI, ALU.mult)
                tt(tc_, aR, bI, ALU.mult)
                tt(td_, aI, bR, ALU.mult)
                tt(oR, ta, tb, ALU.subtract)
                tt(oI, tc_, td_, ALU.add)
            ARt, AIt, LDT = s5in[:, 0, :], s5in[:, 1, :], s5in[:, 2, :]
            DT = snew(); act(DT, LDT, AF.Exp)
            XR = snew(); tt(XR, ARt, DT, ALU.mult)
            XI = snew(); tt(XI, AIt, DT, ALU.mult)
            MAG = snew(); act(MAG, XR, AF.Exp)
            sn = snew(); act(sn, XI, AF.Sin, scale=0.125)
            hs = snew(); act(hs, XI, AF.Sin, scale=0.0625)
            cs = snew(); tt(cs, hs, hs, ALU.mult); ts(cs, cs, -2.0, ALU.mult, 1.0, ALU.add)
            for _ in range(3):
                t1 = snew(); t2 = snew()
                tt(t1, cs, cs, ALU.mult)
                tt(t2, sn, sn, ALU.mult)
                P.c("dve", lambda e, sn=sn, cs=cs: e.scalar_tensor_tensor(out=sn, in0=sn, scalar=2.0, in1=cs, op0=ALU.mult, op1=ALU.mult), [tof(sn), tof(cs)], [tof(sn)])
                tt(cs, t1, t2, ALU.subtract)
                sidx[0] -= 2
            LR = snew(); tt(LR, MAG, cs, ALU.mult)
            LI = snew(); tt(LI, MAG, sn, ALU.mult)
            den = snew(); t1 = snew()
            tt(den, ARt, ARt, ALU.mult); tt(t1, AIt, AIt, ALU.mult); tt(den, den, t1, ALU.add)
            P.c("dve", lambda e: e.reciprocal(out=den, in_=den), [tof(den)], [tof(den)])
            NR = snew(); ts(NR, LR, -1.0, ALU.add)
            CR = snew(); CI = snew()
            tt(CR, NR, ARt, ALU.mult); tt(t1, LI, AIt, ALU.mult); tt(CR, CR, t1, ALU.add); tt(CR, CR, den, ALU.mult)
            tt(CI, LI, ARt, ALU.mult); tt(t1, NR, AIt, ALU.mult); tt(CI, CI, t1, ALU.subtract); tt(CI, CI, den, ALU.mult)
            PRk = [snew() for _ in range(9)]; PIk = [snew() for _ in range(9)]
            P.c("dve", lambda e: e.memset(PRk[0], 1.0), [], [tof(PRk[0])])
            P.c("dve", lambda e: e.memset(PIk[0], 0.0), [], [tof(PIk[0])])
            for k in range(8):
                cmul(PRk[k + 1], PIk[k + 1], PRk[k], PIk[k], LR, LI)
            m2 = snew(); tt(m2, LR, LR, ALU.mult); tt(t1, LI, LI, ALU.mult); tt(m2, m2, t1, ALU.add)
            P.c("dve", lambda e: e.reciprocal(out=m2, in_=m2), [tof(m2)], [tof(m2)])
            IR = snew(); II = snew()
            tt(IR, LR, m2, ALU.mult)
            P.c("dve", lambda e: e.scalar_tensor_tensor(out=II, in0=LI, scalar=-1.0, in1=m2, op0=ALU.mult, op1=ALU.mult), [tof(LI), tof(m2)], [tof(II)])
            NPR = [snew() for _ in range(8)]; NPI = [snew() for _ in range(8)]
            P.c("dve", lambda e: e.memset(NPR[0], 1.0), [], [tof(NPR[0])])
            P.c("dve", lambda e: e.memset(NPI[0], 0.0), [], [tof(NPI[0])])
            for k in range(7):
                cmul(NPR[k + 1], NPI[k + 1], NPR[k], NPI[k], IR, II)
            DRk = [snew() for _ in range(8)]; DIk = [snew() for _ in range(8)]
            for k in range(8):
                cmul(DRk[k], DIk[k], PRk[k], PIk[k], CR, CI)
            P.c("dve", lambda e: e.tensor_copy(out=MU[:, 0, 0, :], in_=PRk[8]), [tof(PRk[8])], [Tmu])
            P.c("dve", lambda e: e.tensor_copy(out=MU[:, 0, 1, :], in_=PIk[8]), [tof(PIk[8])], [Tmu])
            for k in range(6):
                t3 = snew(); t4 = snew()
                P.c("dve", lambda e, k=k, t3=t3: e.tensor_tensor(out=t3, in0=MU[:, k, 0, :], in1=MU[:, k, 0, :], op=ALU.mult), [Tmu], [tof(t3)])
                P.c("dve", lambda e, k=k, t4=t4: e.tensor_tensor(out=t4, in0=MU[:, k, 1, :], in1=MU[:, k, 1, :], op=ALU.mult), [Tmu], [tof(t4)])
                P.c("dve", lambda e, k=k: e.scalar_tensor_tensor(out=MU[:, k + 1, 1, :], in0=MU[:, k, 0, :], scalar=2.0, in1=MU[:, k, 1, :],
                                                                op0=ALU.mult, op1=ALU.mult), [Tmu], [Tmu])
                P.c("dve", lambda e, k=k, t3=t3, t4=t4: e.tensor_tensor(out=MU[:, k + 1, 0, :], in0=t3, in1=t4, op=ALU.subtract), [tof(t3), tof(t4), Tmu], [Tmu])
                sidx[0] -= 2
            for k in range(7):
                P.c("dve", lambda e, k=k: e.tensor_scalar(out=MU[:, k, 2, :], in0=MU[:, k, 1, :], scalar1=-1.0, scalar2=None, op0=ALU.mult), [Tmu], [Tmu])
            s_cw[0] = len(S_ops)
            EE, TE = AR.take([2, 16, 8, 16], F32, "EE")
            Ere, Eim = EE[:, 0], EE[:, 1]
            FF3, TF3 = AR.take([2, 16, 8, 16], F32, "FF3")
            F3re, F3im = FF3[:, 0], FF3[:, 1]
            w1, Tw1 = AR.take([16, 16], F32, "w1")
            w2, Tw2 = AR.take([16, 16], F32, "w2")
            w3, Tw3 = AR.take([16, 16], F32, "w3")
            w4, Tw4 = AR.take([16, 16], F32, "w4")
            BRE, BIM, CRE, CIM = Bt[:, 0], Bt[:, 1], Ct[:, 0], Ct[:, 1]
            Fre4 = Fre[:].rearrange("p a (j h) -> p a j h", h=16)
            Fmim4 = Fmim[:].rearrange("p a (j h) -> p a j h", h=16)

            def cw(oR, oI, sR, sI, mR, mI, neg_im, Tout):
                rd = [tof(sR), tof(sI), TBt, TCt]
                P.c("dve", lambda e: e.tensor_tensor(out=w1, in0=mR, in1=bc(sR, 16), op=ALU.mult), rd, [Tw1])
                P.c("dve", lambda e: e.tensor_tensor(out=w2, in0=mI, in1=bc(sI, 16), op=ALU.mult), rd, [Tw2])
                P.c("dve", lambda e: e.tensor_tensor(out=w3, in0=mI, in1=bc(sR, 16), op=ALU.mult), rd, [Tw3])
                P.c("dve", lambda e: e.tensor_tensor(out=w4, in0=mR, in1=bc(sI, 16), op=ALU.mult), rd, [Tw4])
                P.c("dve", lambda e: e.tensor_tensor(out=oR, in0=w1, in1=w2, op=ALU.subtract), [Tw1, Tw2], [Tout])
                if neg_im:
                    P.c("dve", lambda e: e.scalar_tensor_tensor(out=oI, in0=w3, scalar=-1.0, in1=w4, op0=ALU.mult, op1=ALU.subtract), [Tw3, Tw4], [Tout])
                else:
                    P.c("dve", lambda e: e.tensor_tensor(out=oI, in0=w3, in1=w4, op=ALU.add), [Tw3, Tw4], [Tout])
            for i in range(8):
                cw(Ere[:, :, i, :], Eim[:, :, i, :], DRk[7 - i], DIk[7 - i], BRE, BIM, False, TE)
            for j in range(8):
                cw(Fre4[:, :, j, :], Fmim4[:, :, j, :], PRk[j + 1], PIk[j + 1], CRE, CIM, True, Tff)
                cw(F3re[:, :, j, :], F3im[:, :, j, :], NPR[7 - j], NPI[7 - j], CRE, CIM, True, TF3)
            s_tail[0] = len(S_ops)
            for (Esrc, Wdst) in ((Ere, WSre), (Eim, WSim)):
                for pq in range(4):
                    bk, Tbk = nb()

                    def fn(e, Esrc=Esrc, pq=pq, bk=bk):
                        ins = None
                        for pl in range(4):
                            pi_ = pq * 4 + pl
                            ins = e.transpose(bk[:, pl * 128:(pl + 1) * 128], Esrc[:, pi_].rearrange("p i h -> p (i h)"), ident[:])
                        return ins
                    P.c("pe", fn, [TE, Tid], [Tbk])
                    P.c("act", lambda e, bk=bk, Wdst=Wdst, pq=pq: e.activation(out=Wdst[:, pq * 4:(pq + 1) * 4, :].rearrange("p a c -> p (a c)"),
                                                                               in_=bk, func=AF.Identity), [Tbk], [Tws])
            tkt, Ttkt = AR.take([4, 128], F32, "tkt")
            for gq in range(8):
                bk, Tbk = nb()

                for gl in range(4):
                    def fn(e, gq=gq, bk=bk, gl=gl):
                        g = gq * 4 + gl
                        pi_, ee = g // 2, g % 2
                        sl = slice(64 * ee, 64 * ee + 64)
                        e.matmul(bk[:, gl * 128:(gl + 1) * 128], lhsT=Ere[sl, pi_].rearrange("p i h -> p (i h)"),
                                 rhs=F3re[sl, pi_].rearrange("p i h -> p (i h)"), start=True, stop=False)
                        return e.matmul(bk[:, gl * 128:(gl + 1) * 128], lhsT=Eim[sl, pi_].rearrange("p i h -> p (i h)"),
                                        rhs=F3im[sl, pi_].rearrange("p i h -> p (i h)"), start=False, stop=True)
                    P.c("pe", fn, [TE, TF3], [Tbk])
                P.c("dve", lambda e, bk=bk: e.tensor_tensor(out=tkt, in0=bk.rearrange("p (a c) -> p a c", c=128), in1=bcmid(cmask[:], 4), op=ALU.mult),
                    [Tbk, Tcm], [Ttkt])
                P.c("dve", lambda e, gq=gq: e.tensor_tensor(out=TK[:, gq * 4:(gq + 1) * 4, :], in0=tkt, in1=TK[:, gq * 4:(gq + 1) * 4, :], op=ALU.add),
                    [Ttkt, Ttk], [Ttk])
            assert AR.off <= AR.n, "S5 scratch overflow"
        P.barrier()

        def phase_load_x(hf, base=0, ntiles=8, act_only=False):
            AR.reset(base)
            xts = [AR.take([D], F32, "xt%d" % i) for i in range(ntiles)]
            xts = [xts[i % ntiles] for i in range(8)]
            for ttg in range(2):
                for tl in range(4):
                    tix = ttg * 4 + tl
                    xt, Txt = xts[tix]
                    r0 = hf * HALF + tix * 128
                    P.dma(xt, xin[r0:r0 + 128, :], writes=[Txt], q="sp", sem=1)
                for n in range(8):
                    bk, Tbk = nb()

                    def fn(e, n=n, ttg=ttg, bk=bk):
                        ins = None
                        for tl in range(4):
                            xt = xts[ttg * 4 + tl][0]
                            ins = e.transpose(bk[:, tl * 128:(tl + 1) * 128], xt[:, n * 128:(n + 1) * 128], ident[:])
                        return ins
                    P.c("pe", fn, [xts[ttg * 4 + tl][1] for tl in range(4)] + [Tid], [Tbk])
                    eng = "act" if (n % 2 == 0 or act_only) else "dve"
                    if eng == "act":
                        P.c("act", lambda e, n=n, ttg=ttg, bk=bk: e.activation(out=X[:, n, ttg * 512:(ttg + 1) * 512], in_=bk, func=AF.Identity),
                            [Tbk], [TX[n]])
                    else:
                        P.c("dve", lambda e, n=n, ttg=ttg, bk=bk: e.tensor_copy(out=X[:, n, ttg * 512:(ttg + 1) * 512], in_=bk), [Tbk], [TX[n]])
            P.barrier()

        nstate = {}

        def phase_norm(vsrc, voff, shsrc, shoff, final=False, hf=0, base=0, part="all"):
            if part in ("all", "A"):
                AR.reset(base)
                nstate["sqs"] = [AR.take([8, 512], BF16, "sq%d" % i) for i in range(2)]
                nstate["rss"] = [AR.take([512], F32, "rs%d" % i) for i in range(2)]
                nstate["tmps"] = [AR.take([512], F32, "nt%d" % i) for i in range(3)]
            sqs, rss, tmps = nstate["sqs"], nstate["rss"], nstate["tmps"]
            if final:
                ofs = [AR.take([8, 512], F32, "of%d" % i) for i in range(2)]
                ots = [AR.take([D], F32, "ot%d" % i) for i in range(3)]
            oti = 0
            toks = (slice(0, 512), slice(512, 1024))
            for th in (range(2) if part in ("all", "A") else ()):
                sq, Tsq = sqs[th]
                for n in range(8):
                    P.c("act", lambda e, n=n, sq=sq, tok=toks[th]: e.activation(out=sq[:, n, :], in_=X[:, n, tok], func=AF.Square), [TX[n]], [Tsq])
            for th in (range(2) if part in ("all", "A") else ()):
                sq, Tsq = sqs[th]
                rs, Trs = rss[th]
                bk, Tbk = nb()

                def fn(e, sq=sq, bk=bk):
                    ins = None
                    for n in range(8):
                        ins = e.matmul(bk, lhsT=onesm[:], rhs=sq[:, n, :], start=(n == 0), stop=(n == 7))
                    return ins
                P.c("pe", fn, [Tsq, Tones], [Tbk])
                P.c("act", lambda e, rs=rs, bk=bk: e.activation(out=rs, in_=bk, func=AF.Ln, bias=vcol(V_EPS)), [Tbk, Tvec], [Trs])
                P.c("act", lambda e, rs=rs: e.activation(out=rs, in_=rs, func=AF.Exp, scale=-0.5), [Trs], [Trs])
            for th in (range(2) if part in ("all", "B") else ()):
                tok = toks[th]
                rs, Trs = rss[th]
                for n in range(8):
                    if not final:
                        tmp, Ttmp = tmps[n % 3]
                        P.c("dve", lambda e, n=n, tmp=tmp, rs=rs, tok=tok: e.scalar_tensor_tensor(
                            out=tmp, in0=X[:, n, tok], scalar=vsrc[:, voff + n:voff + n + 1], in1=rs, op0=ALU.mult, op1=ALU.mult),
                            [TX[n], Trs, Tcv, Tvec], [Ttmp])
                        P.c("act", lambda e, n=n, tmp=tmp, tok=tok: e.activation(out=U[:, n, tok], in_=tmp, func=AF.Identity,
                                                                                 bias=shsrc[:, shoff + n:shoff + n + 1]), [Ttmp, Tcv], [TU])
                    else:
                        of, Tof = ofs[th]
                        P.c("dve", lambda e, n=n, rs=rs, tok=tok, of=of: e.scalar_tensor_tensor(
                            out=of[:, n, :], in0=X[:, n, tok], scalar=vec[:, V_FG + n:V_FG + n + 1], in1=rs, op0=ALU.mult, op1=ALU.mult),
                            [TX[n], Trs, Tvec], [Tof])
                if final:
                    of, Tof = ofs[th]
                    for tl in range(4):
                        ot, Tot = ots[oti % 3]
                        oti += 1
                        for nh in range(2):
                            bk, Tbk = nb()

                            def fn(e, nh=nh, tl=tl, bk=bk, of=of):
                                ins = None
                                for nl in range(4):
                                    ins = e.transpose(bk[:, nl * 128:(nl + 1) * 128], of[:, nh * 4 + nl, tl * 128:(tl + 1) * 128], ident[:])
                                return ins
                            P.c("pe", fn, [Tof, Tid], [Tbk])
                            if nh == 0:
                                P.c("act", lambda e, ot=ot, bk=bk: e.activation(out=ot[:, 0:512], in_=bk, func=AF.Identity), [Tbk], [Tot])
                            else:
                                P.c("dve", lambda e, ot=ot, bk=bk: e.tensor_copy(out=ot[:, 512:1024], in_=bk), [Tbk], [Tot])
                        r0 = hf * HALF + th * 512 + tl * 128
                        P.dma(out[r0:r0 + 128, :], ot, reads=[Tot], q="sp", sem=2)
            P.barrier()

        ffn_calls = [0]

        def phase_ffn(wup, wdn, goff):
            first = (ffn_calls[0] == 0)
            modg = {0: (8, 20, (2, 3, 4)), 1: (32, 36, (8,))}.get(ffn_calls[0])
            ffn_calls[0] += 1
            AR.reset()
            hid, Th0 = AR.take([NKF, HALF], BF16, "hid")
            Thid = [AR.child(Th0, "hid%d" % k) for k in range(NKF)]
            wus = [AR.take([8, 2, 128], BF16, "wu%d" % i) for i in range(3)]
            wusb = [AR.child(w_[1], "wub") for w_ in wus]
            sls = [AR.take([512], BF16, "sl%d" % i) for i in range(3)]
            if modg:
                mstate["bufs"] = [AR.take([8, 256], BF16, "fmw%d" % i) for i in range(2)]
                mstate["i"] = 0
            assert AR.off <= S_OFF
            AR.reset(S_OFF)
            wd, Tw0 = AR.take([NKF, D], BF16, "wd")
            Twd = [AR.child(Tw0, "wd%d" % k) for k in range(NKF)]
            wup_v = wup.rearrange("(kt kp) c -> kp kt c", kp=128)
            wdn_v = wdn.rearrange("(kt kp) c -> kp kt c", kp=128)
            si = 0
            pend = []
            nextg = [modg[0] if modg else 0]
            endg = modg[1] if modg else 0
            NFL = 16
            n_tail = (s_tail_len[0]) if first else 0
            n_head = (len(S_ops) - n_tail) if first else 0
            head_left = [n_head]
            wdq = list(range(NKF))

            def wu_dma(kk):
                wu_, Twu_ = wus[kk % 3]
                P.dma(wu_[:, :, 0, :], wup_v[:, :, kk * 128:(kk + 1) * 128], writes=[Twu_], q="pool", sem=0)
                P.dma(wu_[:, :, 1, :], wup_v[:, :, DFF + kk * 128:DFF + (kk + 1) * 128], writes=[wusb[kk % 3]], q="pool", sem=0)
            wu_dma(0)
            wu_dma(1)
            for k in range(NKF):
                wu, Twu = wus[k % 3]
                Twub = wusb[k % 3]
                if modg:
                    for ent in pend:
                        mod_mm(ent)
                    pend = []
                    for _ in range(2):
                        if nextg[0] < endg:
                            pend.append(mod_dma(nextg[0]))
                            nextg[0] += 1
                if k + 2 < NKF:
                    wu_dma(k + 2)
                if not first:
                    P.dma(wd[:, k, :], wdn_v[:, k, :], writes=[Twd[k]], q="pool", sem=0)
                elif k >= NFL:
                    assert len(S_ops) == 0
                    for _ in range(4):
                        if wdq:
                            kk = wdq.pop(0)
                            P.dma(wd[:, kk, :], wdn_v[:, kk, :], writes=[Twd[kk]], q="pool", sem=0)
                for th in range(2):
                    tok = slice(th * 512, (th + 1) * 512)
                    bka, Tbka = nb()
                    bkb, Tbkb = nb()

                    def fn(e, wu=wu, bka=bka, bkb=bkb, tok=tok):
                        ins = None
                        for kt in range(8):
                            e.matmul(bka, lhsT=wu[:, kt, 0, :], rhs=U[:, kt, tok], start=(kt == 0), stop=(kt == 7))
                        for kt in range(8):
                            ins = e.matmul(bkb, lhsT=wu[:, kt, 1, :], rhs=U[:, kt, tok], start=(kt == 0), stop=(kt == 7))
                        return ins
                    P.c("pe", fn, [Twu, Twub, TU], [Tbka, Tbkb])
                    sl, Tsl = sls[si % 3]
                    si += 1
                    P.c("act", lambda e, sl=sl, bka=bka: e.activation(out=sl, in_=bka, func=AF.Silu), [Tbka], [Tsl])
                    P.c("dve", lambda e, sl=sl, bkb=bkb, k=k, tok=tok: e.tensor_tensor(out=hid[:, k, tok], in0=sl, in1=bkb, op=ALU.mult),
                        [Tsl, Tbkb], [Thid[k]])
                if first:
                    if k < 11:
                        nf = min((n_head + 10) // 11, head_left[0])
                        head_left[0] -= nf
                        if nf:
                            P.flush(S_ops, nf)
                    elif 13 <= k < NFL:
                        P.flush(S_ops, (n_tail + 2) // 3 if k < NFL - 1 else len(S_ops))
            if modg:
                for ent in pend:
                    mod_mm(ent)
                assert nextg[0] == endg and len(S_ops) == 0
                for m in modg[2]:
                    mod_finish(m)
            for n in range(8):
                bks = [nb() for _ in range(2)]

                def fn(e, n=n, bks=bks):
                    ins = None
                    for kt in range(NKF):
                        for th in range(2):
                            ins = e.matmul(bks[th][0], lhsT=wd[:, kt, n * 128:(n + 1) * 128], rhs=hid[:, kt, th * 512:(th + 1) * 512],
                                           start=(kt == 0), stop=(kt == NKF - 1))
                    return ins
                P.c("pe", fn, Twd + Thid, [bks[0][1], bks[1][1]])
                for th in range(2):
                    tok = slice(th * 512, (th + 1) * 512)
                    bk, Tbk = bks[th]
                    P.c("dve", lambda e, n=n, bk=bk, tok=tok: e.scalar_tensor_tensor(
                        out=X[:, n, tok], in0=bk, scalar=cv[:, goff + n:goff + n + 1], in1=X[:, n, tok], op0=ALU.mult, op1=ALU.add),
                        [Tbk, Tcv, TX[n]], [TX[n]])
            P.barrier()

        win_v = win.rearrange("(kt kp) c -> kp kt c", kp=128)

        def phase_mixer(hf):
            AR.reset()
            ya, Tya = AR.take([8, HALF], BF16, "ya")
            yb, Tyb = AR.take([4, HALF], BF16, "yb")
            base = AR.off
            XT, TXT = AR.take([32, 128], BF16, "XT")
            Hs, THs0 = AR.take([16, 2, 128], BF16, "Hs")
            THs = [AR.child(THs0, "Hs%d" % i) for i in range(16)]
            scratch = AR.off
            Zs = []
            for i in range(4):
                za, Tza = AR.take([2, 192], F32, "za%d" % i)
                zb, Tzb = AR.take([2, 192], F32, "zb%d" % i)
                Zs.append(((za, Tza), (zb, Tzb)))
                P.c("pool", lambda e, za=za: e.memset(za[:, :, 0:64], 0.0), [], [Tza])
                P.c("pool", lambda e, zb=zb: e.memset(zb[:, :, 0:64], 0.0), [], [Tzb])
            wts = [AR.take([8, 128], BF16, "wt%d" % i) for i in range(4)]
            wti = [0]

            def load_wt(col0):
                wt, Twt = wts[wti[0] % 4]
                wti[0] += 1
                P.dma(wt, win_v[:, :, col0:col0 + 128], writes=[Twt], q="pool", sem=0)
                return wt, Twt
            def mkset():
                dct = {}
                dct["xap"] = AR.take([HALF + 8], BF16, "xap")
                dct["xc"] = AR.take([HALF], BF16, "xc")
                dct["ra"] = AR.take([HALF], F32, "ra")
                dct["ig"] = AR.take([HALF], F32, "ig")
                dct["ml"] = AR.take([HALF], F32, "ml")
                dct["u"] = AR.take([HALF], F32, "u")
                dct["gg"] = AR.take([HALF], F32, "gg")
                return dct
            sets = [mkset()]
            set1_off = AR.off
            ths = (slice(0, 512), slice(512, 1024))

            def ks_pairs(pis):
                st_ = []
                for pi_ in pis:
                    (za, Tza), (zb, Tzb) = Zs[pi_ % 4]
                    bk, Tbk = nb()

                    def fn(e, pi_=pi_, bk=bk):
                        ins = None
                        for ee in range(2):
                            sl = slice(64 * ee, 64 * ee + 64)
                            e.matmul(bk[sl, 0:128], lhsT=WSre[:, pi_, sl], rhs=XT[:, 2 * pi_ + ee, :], start=True, stop=True)
                            ins = e.matmul(bk[sl, 128:256], lhsT=WSim[:, pi_, sl], rhs=XT[:, 2 * pi_ + ee, :], start=True, stop=True)
                        return ins
                    P.c("pe", fn, [Tws, TXT], [Tbk])
                    P.c("act", lambda e, za=za, bk=bk: e.activation(out=za[:, :, 64:192], in_=bk[:, 0:256].rearrange("p (a c) -> p a c", c=128),
                                                                    func=AF.Identity), [Tbk], [Tza])
                    st_.append([pi_, (za, Tza), (zb, Tzb)])
                if hf > 0:
                    for step in range(2):
                        for (pi_, (za, Tza), _) in st_:
                            mr = MU[:, 0, 0, pi_:pi_ + 1]; mi = MU[:, 0, 1, pi_:pi_ + 1]; nmi = MU[:, 0, 2, pi_:pi_ + 1]
                            hr = Hin[:, 0, pi_:pi_ + 1]; hi_ = Hin[:, 1, pi_:pi_ + 1]
                            for (dst, src, m) in (((0, hr, mr), (1, hi_, mr)) if step == 0 else ((0, hi_, nmi), (1, hr, mi))):
                                P.c("dve", lambda e, za=za, dst=dst, src=src, m=m: e.scalar_tensor_tensor(
                                    out=za[:, dst, 64:65], in0=src, scalar=m, in1=za[:, dst, 64:65], op0=ALU.mult, op1=ALU.add), [Tza, Thin, Tmu], [Tza])
                for k in range(7):
                    s_ = 1 << k
                    for ent in st_:
                        pi_, (src, Tsrc), (dst, Tdst) = ent
                        mr = MU[:, k, 0, pi_:pi_ + 1]
                        P.c("dve", lambda e, src=src, dst=dst, mr=mr, s_=s_: e.scalar_tensor_tensor(
                            out=dst[:, :, 64:192], in0=src[:, :, 64 - s_:192 - s_], scalar=mr, in1=src[:, :, 64:192],
                            op0=ALU.mult, op1=ALU.add), [Tsrc, Tmu], [Tdst])
                    for (dc, sc2, mix) in ((0, 1, 2), (1, 0, 1)):
                        for ent in st_:
                            pi_, (src, Tsrc), (dst, Tdst) = ent
                            m2 = MU[:, k, mix, pi_:pi_ + 1]
                            P.c("dve", lambda e, src=src, dst=dst, dc=dc, sc2=sc2, m2=m2, s_=s_: e.scalar_tensor_tensor(
                                out=dst[:, dc, 64:192], in0=src[:, sc2, 64 - s_:192 - s_], scalar=m2, in1=dst[:, dc, 64:192],
                                op0=ALU.mult, op1=ALU.add), [Tsrc, Tmu, Tdst], [Tdst])
                    for ent in st_:
                        ent[1], ent[2] = ent[2], ent[1]
                for (pi_, (zf, Tzf), _) in st_:
                    P.c("dve", lambda e, pi_=pi_, zf=zf: e.tensor_copy(out=Hs[:, pi_, :, 1:128], in_=zf[:, :, 64:191]), [Tzf], [THs[pi_]])
                    P.c("dve", lambda e, pi_=pi_: e.tensor_copy(out=Hs[:, pi_, :, 0:1], in_=Hin[:, :, pi_:pi_ + 1]), [Thin], [THs[pi_]])
                    P.c("dve", lambda e, pi_=pi_, zf=zf: e.tensor_copy(out=Hin[:, :, pi_:pi_ + 1], in_=zf[:, :, 191:192]),
                        [Tzf, THs[pi_]], [Thin])

            wtl = {}

            def front(q):
                S_ = sets[q % 2]
                xap, Txap = S_["xap"]; xc, Txc = S_["xc"]; ra, Tra = S_["ra"]; ig, Tig = S_["ig"]
                ml, Tml = S_["ml"]
                wt, Twt = load_wt(q * 128)
                wtl[q] = load_wt(1024 + q * 128)
                P.c("act", lambda e, q=q, xap=xap: e.activation(out=xap[:, 0:4], in_=xatail[:, q, :], func=AF.Identity), [Txat], [Txap])
                for th in range(2):
                    tok = ths[th]
                    bk, Tbk = nb()

                    def fn(e, wt=wt, bk=bk, tok=tok):
                        ins = None
                        for kt in range(8):
                            ins = e.matmul(bk, lhsT=wt[:, kt, :], rhs=U[:, kt, tok], start=(kt == 0), stop=(kt == 7))
                        return ins
                    P.c("pe", fn, [Twt, TU], [Tbk])
                    P.c("act", lambda e, q=q, bk=bk, xap=xap, th=th: e.activation(out=xap[:, 4 + th * 512:4 + (th + 1) * 512], in_=bk, func=AF.Identity,
                                                                                 bias=vcol(V_BIN, q)), [Tbk, Tvec], [Txap])
                P.c("act", lambda e, q=q, xap=xap: e.activation(out=xatail[:, q, :], in_=xap[:, HALF:HALF + 4], func=AF.Identity), [Txap], [Txat])
                gbanks = []
                for th in range(2):
                    tok = ths[th]
                    bk, Tbk = nb()

                    def fn(e, q=q, bk=bk, xap=xap, th=th):
                        ins = None
                        for k in range(4):
                            ins = e.matmul(bk, lhsT=cdiag[:, q, k, :], rhs=xap[:, 1 + th * 512 + k:1 + th * 512 + k + 512], start=(k == 0), stop=(k == 3))
                        return ins
                    P.c("pe", fn, [Tcd, Txap], [Tbk])
                    P.c("act", lambda e, q=q, bk=bk, xc=xc, tok=tok: e.activation(out=xc[:, tok], in_=bk, func=AF.Identity, bias=vcol(V_CONVB, q)),
                        [Tbk, Tvec], [Txc])
                    bkr, Tbkr = nb()
                    bki, Tbki = nb()

                    def fn2(e, q=q, bkr=bkr, bki=bki, xc=xc, tok=tok):
                        e.matmul(bkr, lhsT=WR[:, q, :], rhs=xc[:, tok], start=True, stop=True)
                        return e.matmul(bki, lhsT=WI[:, q, :], rhs=xc[:, tok], start=True, stop=True)
                    P.c("pe", fn2, [Twri, Txc], [Tbkr, Tbki])
                    gbanks.append((bkr, Tbkr, bki, Tbki))
                for th in range(2):
                    tok = ths[th]
                    bkr, Tbkr, bki, Tbki = gbanks[th]
                    P.c("act", lambda e, q=q, bkr=bkr, ra=ra, tok=tok: e.activation(out=ra[:, tok], in_=bkr, func=AF.Sigmoid, bias=vcol(V_LBR, q)),
                        [Tbkr, Tvec], [Tra])
                    P.c("act", lambda e, q=q, bki=bki, ig=ig, tok=tok: e.activation(out=ig[:, tok], in_=bki, func=AF.Sigmoid, bias=vcol(V_LBI, q)),
                        [Tbki, Tvec], [Tig])
                P.c("act", lambda e, q=q, ra=ra, ml=ml: e.activation(out=ml, in_=ra, func=AF.Exp, scale=ccol(C_C2, q)), [Tra, Tcv], [Tml])
                P.c("act", lambda e, q=q, ra=ra: e.activation(out=ra, in_=ra, func=AF.Exp, scale=ccol(C_C1, q)), [Tra, Tcv], [Tra])
                P.c("act", lambda e, ml=ml: e.activation(out=ml, in_=ml, func=AF.Sqrt, scale=-1.0, bias=vcol(V_ONE)), [Tml, Tvec], [Tml])

            def back(q):
                S_ = sets[q % 2]
                xc, Txc = S_["xc"]; ra, Tra = S_["ra"]; ig, Tig = S_["ig"]
                ml, Tml = S_["ml"]; u_, Tu = S_["u"]; gg, Tgg = S_["gg"]
                wtg, Twtg = wtl[q]
                P.c("dve", lambda e, ig=ig, xc=xc, u_=u_: e.tensor_tensor(out=u_, in0=ig, in1=xc, op=ALU.mult), [Tig, Txc], [Tu])
                P.c("dve", lambda e, ml=ml, u_=u_: e.tensor_tensor(out=u_, in0=ml, in1=u_, op=ALU.mult), [Tml, Tu], [Tu])
                P.c("dve", lambda e, q=q, ra=ra, u_=u_: e.tensor_tensor_scan(out=u_, data0=ra, data1=u_, initial=hcar[:, q:q + 1],
                                                                            op0=ALU.mult, op1=ALU.add), [Tra, Tu, Thc], [Tu])
                P.c("dve", lambda e, q=q, u_=u_: e.tensor_copy(out=hcar[:, q:q + 1], in_=u_[:, HALF - 1:HALF]), [Tu], [Thc])
                for th in range(2):
                    tok = ths[th]
                    bk, Tbk = nb()

                    def fn(e, wtg=wtg, bk=bk, tok=tok):
                        ins = None
                        for kt in range(8):
                            ins = e.matmul(bk, lhsT=wtg[:, kt, :], rhs=U[:, kt, tok], start=(kt == 0), stop=(kt == 7))
                        return ins
                    P.c("pe", fn, [Twtg, TU], [Tbk])
                    P.c("act", lambda e, q=q, bk=bk, gg=gg, tok=tok: e.activation(out=gg[:, tok], in_=bk, func=AF.Gelu_apprx_tanh, bias=vcol(V_BIN, 8 + q)),
                        [Tbk, Tvec], [Tgg])

            def back2(q):
                S_ = sets[q % 2]
                u_, Tu = S_["u"]; gg, Tgg = S_["gg"]
                P.c("dve", lambda e, q=q, gg=gg, u_=u_: e.tensor_tensor(out=ya[:, q, :], in0=u_, in1=gg, op=ALU.mult), [Tgg, Tu], [Tya])

            front(0)
            AR.reset(set1_off)
            Xc2, TXc2 = AR.take([32, 8, 16], BF16, "Xc2")
            wxb, Twxb = AR.take([8, 512], BF16, "wxb")
            P.dma(wxb, win_v[:, :, 2048:2560], writes=[Twxb], q="pool", sem=0)
            mq = []
            if hf == 0:
                mstate["bufs"] = [AR.take([8, 256], BF16, "bmw%d" % i) for i in range(3)]
                mstate["i"] = 0
                mnext = [20]
                for _ in range(3):
                    mq.append(mod_dma(mnext[0]))
                    mnext[0] += 1

            def mod_pump():
                if mq:
                    mod_mm(mq.pop(0))
                    if mnext[0] < 32:
                        mq.append(mod_dma(mnext[0]))
                        mnext[0] += 1
            for i in range(8):
                mod_pump()
                bk, Tbk = nb()

                def fn(e, i=i, bk=bk):
                    ins = None
                    for kt in range(8):
                        ins = e.matmul(bk, lhsT=U[:, kt, :].rearrange("p (c i) -> p i c", i=8)[:, i, :], rhs=wxb[:, kt, :],
                                       start=(kt == 0), stop=(kt == 7))
                    return ins
                P.c("pe", fn, [TU, Twxb], [Tbk])
                P.c("dve", lambda e, i=i, bk=bk: e.tensor_tensor(out=Xc2[:, :, i, :], in0=bk.rearrange("p (g h) -> p g h", h=16),
                                                                 in1=bxb[:].rearrange("p (g h) -> p g h", h=16), op=ALU.add), [Tbk, Tbxb], [TXc2])
            for gq in range(8):
                mod_pump()
                bk, Tbk = nb()
                bkb = bk.bitcast(BF16)

                def fn(e, gq=gq, bkb=bkb):
                    ins = None
                    for gl in range(4):
                        ins = e.transpose(bkb[:, gl * 128:(gl + 1) * 128], Xc2[:, gq * 4 + gl].rearrange("p i h -> p (i h)"), identb[:])
                    return ins
                P.c("pe", fn, [TXc2, Tidb], [Tbk])
                if gq % 2 == 0:
                    P.c("act", lambda e, gq=gq, bkb=bkb: e.activation(out=XT[:, gq * 4:(gq + 1) * 4, :].rearrange("p a c -> p (a c)"),
                                                                      in_=bkb[:, 0:512], func=AF.Identity), [Tbk], [TXT])
                else:
                    P.c("dve", lambda e, gq=gq, bkb=bkb: e.tensor_copy(out=XT[:, gq * 4:(gq + 1) * 4, :].rearrange("p a c -> p (a c)"),
                                                                       in_=bkb[:, 0:512]), [Tbk], [TXT])
            if hf == 0:
                while mq:
                    mod_pump()
                for m in (5, 6, 7):
                    mod_finish(m)
            AR.reset(set1_off)
            sets.append(mkset())
            for q in range(8):
                if q < 7:
                    front(q + 1)
                back(q)
                ks_pairs([2 * q, 2 * q + 1])
                back2(q)
            P.barrier()
            AR.reset(scratch)
            Yg, TYg = AR.take([8, 512], BF16, "Yg")
            ygT, TygT = AR.take([4, HALF], BF16, "ygT")
            gw, Tgw = AR.take([4, 512], BF16, "gw")
            sgs = [AR.take([512], BF16, "sg%d" % i) for i in range(2)]
            P.dma(gw, gluw_d.rearrange("(kt kp) c -> kp kt c", kp=128), writes=[Tgw], q="pool", sem=0)
            W_OFF = 16384
            assert AR.off <= W_OFF
            AR.reset(W_OFF)
            pja, Tpja0 = AR.take([8, D], BF16, "pja"); Tpja = [AR.child(Tpja0, "pja%d" % k) for k in range(8)]
            pjb, Tpjb0 = AR.take([4, D], BF16, "pjb"); Tpjb = [AR.child(Tpjb0, "pjb%d" % k) for k in range(4)]
            wo, Two0 = AR.take([8, D], BF16, "wo"); Two = [AR.child(Two0, "wo%d" % k) for k in range(8)]
            pja_v = pja_d.rearrange("(kt kp) c -> kp kt c", kp=128)
            pjb_v = pjb_d.rearrange("(kt kp) c -> kp kt c", kp=128)
            wo_v = wo_d.rearrange("(kt kp) c -> kp kt c", kp=128)
            for kt in range(8):
                P.dma(pja[:, kt, :], pja_v[:, kt, :], writes=[Tpja[kt]], q="pool", sem=0)
            for kt in range(4):
                P.dma(pjb[:, kt, :], pjb_v[:, kt, :], writes=[Tpjb[kt]], q="pool", sem=0)
            for kt in range(8):
                P.dma(wo[:, kt, :], wo_v[:, kt, :], writes=[Two[kt]], q="pool", sem=0)
            for gq in range(8):
                bk, Tbk = nb()

                def fn(e, gq=gq, bk=bk):
                    ins = None
                    for gl in range(4):
                        g = gq * 4 + gl
                        pi_, ee = g // 2, g % 2
                        sl = slice(64 * ee, 64 * ee + 64)
                        o = bk[:, gl * 128:(gl + 1) * 128]
                        e.matmul(o, lhsT=XT[:, g, :], rhs=TK[:, g, :], start=True, stop=False)
                        e.matmul(o, lhsT=Hs[sl, pi_, 0, :], rhs=Fre[sl, pi_, :], start=False, stop=False)
                        ins = e.matmul(o, lhsT=Hs[sl, pi_, 1, :], rhs=Fmim[sl, pi_, :], start=False, stop=True)
                    return ins
                P.c("pe", fn, [TXT, Ttk, Tff] + THs, [Tbk])
                P.c("act", lambda e, gq=gq, bk=bk: e.activation(
                    out=Yg[:, :, gq * 64:(gq + 1) * 64].rearrange("p j (a h) -> p a j h", a=4),
                    in_=bk.rearrange("p (a j h) -> p a j h", a=4, j=8), func=AF.Gelu_apprx_tanh), [Tbk], [TYg])
            for q in range(4):
                for jh in range(2):
                    bk, Tbk = nb()
                    bkb = bk.bitcast(BF16)

                    def fn(e, q=q, jh=jh, bkb=bkb):
                        ins = None
                        for jl in range(4):
                            ins = e.transpose(bkb[:, jl * 128:(jl + 1) * 128], Yg[:, jh * 4 + jl, q * 128:(q + 1) * 128], identb[:])
                        return ins
                    P.c("pe", fn, [TYg, Tidb], [Tbk])
                    P.c("dve", lambda e, q=q, jh=jh, bkb=bkb: e.tensor_copy(out=ygT[:, q, jh * 512:(jh + 1) * 512], in_=bkb[:, 0:512]),
                        [Tbk], [TygT])
            si = 0
            for q in range(4):
                for th in range(2):
                    tok = ths[th]
                    bk, Tbk = nb()

                    def fn(e, q=q, bk=bk, tok=tok):
                        ins = None
                        for kt in range(4):
                            ins = e.matmul(bk, lhsT=gw[:, kt, q * 128:(q + 1) * 128], rhs=ygT[:, kt, tok], start=(kt == 0), stop=(kt == 3))
                        return ins
                    P.c("pe", fn, [Tgw, TygT], [Tbk])
                    sg, Tsg = sgs[si % 2]
                    si += 1
                    P.c("act", lambda e, q=q, bk=bk, sg=sg: e.activation(out=sg, in_=bk, func=AF.Sigmoid, bias=vcol(V_GLUB, q)), [Tbk, Tvec], [Tsg])
                    P.c("dve", lambda e, q=q, sg=sg, tok=tok: e.tensor_tensor(out=yb[:, q, tok], in0=ygT[:, q, tok], in1=sg, op=ALU.mult),
                        [Tsg, TygT], [Tyb])
            P.barrier()
            AR.reset(base)
            m_, Tm = AR.take([8, HALF], BF16, "m")
            wts = [AR.take([8, 128], BF16, "mwt%d" % i) for i in range(4)]
            sas = [AR.take([512], F32, "sa%d" % i) for i in range(2)]
            sbs = [AR.take([512], F32, "sb%d" % i) for i in range(2)]
            t1s = [AR.take([512], F32, "t1%d" % i) for i in range(2)]
            t2s = [AR.take([512], F32, "t2%d" % i) for i in range(2)]
            assert AR.off <= W_OFF
            ci = 0
            for n in range(8):
                wta, Twta = load_wt(2560 + n * 128)
                wtb, Twtb = load_wt(3584 + n * 128)
                for th in range(2):
                    tok = slice(th * 512, (th + 1) * 512)
                    bka, Tbka = nb(); bkb, Tbkb = nb(); bkc, Tbkc = nb(); bkd, Tbkd = nb()

                    def fn(e, n=n, bka=bka, bkb=bkb, tok=tok, th=th):
                        ins = None
                        for kt in range(8):
                            e.matmul(bka, lhsT=pja[:, kt, n * 128:(n + 1) * 128], rhs=ya[:, kt, tok], start=(kt == 0), stop=(kt == 7))
                        for kt in range(4):
                            ins = e.matmul(bkb, lhsT=pjb[:, kt, n * 128:(n + 1) * 128],
                                           rhs=yb[:, kt, :].rearrange("p (j c) -> p j c", j=8)[:, :, th * 64:(th + 1) * 64],
                                           start=(kt == 0), stop=(kt == 3))
                        return ins
                    P.c("pe", fn, Tpja + Tpjb + [Tya, Tyb], [Tbka, Tbkb])

                    def fn2(e, wta=wta, wtb=wtb, bkc=bkc, bkd=bkd, tok=tok):
                        ins = None
                        for kt in range(8):
                            e.matmul(bkc, lhsT=wta[:, kt, :], rhs=U[:, kt, tok], start=(kt == 0), stop=(kt == 7))
                        for kt in range(8):
                            ins = e.matmul(bkd, lhsT=wtb[:, kt, :], rhs=U[:, kt, tok], start=(kt == 0), stop=(kt == 7))
                        return ins
                    P.c("pe", fn2, [Twta, Twtb, TU], [Tbkc, Tbkd])
                    sa, Tsa = sas[ci % 2]; sb_, Tsb = sbs[ci % 2]; t1_, Tt1 = t1s[ci % 2]; t2_, Tt2 = t2s[ci % 2]
                    ci += 1
                    P.c("act", lambda e, n=n, bkc=bkc, sa=sa: e.activation(out=sa, in_=bkc, func=AF.Sigmoid, bias=vcol(V_BIN, 20 + n)), [Tbkc, Tvec], [Tsa])
                    P.c("act", lambda e, n=n, bkd=bkd, sb_=sb_: e.activation(out=sb_, in_=bkd, func=AF.Sigmoid, bias=vcol(V_BIN, 28 + n)), [Tbkd, Tvec], [Tsb])
                    P.c("dve", lambda e, sa=sa, bka=bka, t1_=t1_: e.tensor_tensor(out=t1_, in0=sa, in1=bka, op=ALU.mult), [Tsa, Tbka], [Tt1])
                    P.c("dve", lambda e, sb_=sb_, bkb=bkb, t2_=t2_: e.tensor_tensor(
                        out=t2_.rearrange("p (c j) -> p c j", j=8), in0=sb_.rearrange("p (c j) -> p c j", j=8),
                        in1=bkb.rearrange("p (j c) -> p c j", j=8), op=ALU.mult), [Tsb, Tbkb], [Tt2])
                    P.c("dve", lambda e, n=n, t1_=t1_, t2_=t2_, tok=tok: e.tensor_tensor(out=m_[:, n, tok], in0=t1_, in1=t2_, op=ALU.add), [Tt1, Tt2], [Tm])
            for n in range(8):
                for th in range(2):
                    tok = slice(th * 512, (th + 1) * 512)
                    bk, Tbk = nb()

                    def fn(e, n=n, bk=bk, tok=tok):
                        ins = None
                        for kt in range(8):
                            ins = e.matmul(bk, lhsT=wo[:, kt, n * 128:(n + 1) * 128], rhs=m_[:, kt, tok], start=(kt == 0), stop=(kt == 7))
                        return ins
                    P.c("pe", fn, Two + [Tm], [Tbk])
                    P.c("dve", lambda e, n=n, bk=bk, tok=tok: e.scalar_tensor_tensor(
                        out=X[:, n, tok], in0=bk, scalar=cv[:, 40 + n:40 + n + 1], in1=X[:, n, tok], op0=ALU.mult, op1=ALU.add),
                        [Tbk, Tcv, TX[n]], [TX[n]])
            P.barrier()

        sched = []
        P.deferq = S_ops
        phase_S()
        P.deferq = None
        n_tail = len(S_ops) - s_tail[0]
        s_tail_len[0] = n_tail
        P.flush(S_ops, 60)
        mod_first()
        phase_load_x(0, base=3200, ntiles=4, act_only=True)
        phase_norm(cv, C_V1, cv, 0, base=4608, part="A")
        P.flush(S_ops, s_cw[0] - 60)
        mod_finish(0)
        mod_finish(1)
        phase_norm(cv, C_V1, cv, 0, part="B")
        for hf in range(2):
            if hf > 0:
                sched.append(lambda hf=hf: phase_load_x(hf))
                sched.append(lambda: phase_norm(cv, C_V1, cv, 0))
            sched.append(lambda: phase_ffn(wup1, wdn1, C_G1H))
            sched.append(lambda: phase_norm(cv, C_V2, cv, 24))
            sched.append(lambda hf=hf: phase_mixer(hf))
            sched.append(lambda: phase_norm(cv, C_V3, cv, 48))
            sched.append(lambda: phase_ffn(wup2, wdn2, C_G3H))
            sched.append(lambda hf=hf: phase_norm(None, 0, None, 0, final=True, hf=hf))
        for f in sched[:STOP]:
            f()
        P.emit()
    return nc


_NC_CACHE = {}


def _prep_shared(inp):
    f = np.float32
    sh = {}
    sh["modw"] = np.ascontiguousarray(inp["mod_w"][0], dtype=f)
    sh["wup1"] = np.ascontiguousarray(inp["ffn1_w_up"][0], dtype=f)
    sh["wdn1"] = np.ascontiguousarray(inp["ffn1_w_down"][0], dtype=f)
    sh["wup2"] = np.ascontiguousarray(inp["ffn2_w_up"][0], dtype=f)
    sh["wdn2"] = np.ascontiguousarray(inp["ffn2_w_down"][0], dtype=f)
    sh["win"] = np.ascontiguousarray(inp["w_in"][0], dtype=f)
    sh["pja"] = np.ascontiguousarray(inp["proj_a"][0], dtype=f)
    sh["pjb"] = np.ascontiguousarray(inp["proj_b"][0], dtype=f)
    sh["wo"] = np.ascontiguousarray(inp["w_out"][0], dtype=f)
    sh["gluw"] = np.ascontiguousarray(inp["glu_w"][0], dtype=f)

    def pt(v):
        v = np.asarray(v, dtype=f)
        return np.ascontiguousarray(v.reshape(-1, 128).T)
    vecs = np.zeros((128, NV), f)
    vecs[:, V_MODB:V_MODB + 72] = pt(inp["mod_b"][0])
    vecs[:, V_N1G:V_N1G + 8] = pt(inp["norm1_g"][0])
    vecs[:, V_N2G:V_N2G + 8] = pt(inp["norm2_g"][0])
    vecs[:, V_N3G:V_N3G + 8] = pt(inp["norm3_g"][0])
    vecs[:, V_FG:V_FG + 8] = pt(inp["final_g"])
    vecs[:, V_BIN:V_BIN + 36] = pt(inp["b_in"][0])
    cw = np.asarray(inp["conv_w"][0], dtype=f)
    vecs[:, V_CONVW:V_CONVW + 32] = cw.reshape(4, 8, 128).transpose(2, 1, 0).reshape(128, 32)
    vecs[:, V_CONVB:V_CONVB + 8] = pt(inp["conv_b"][0])
    vecs[:, V_LBR:V_LBR + 8] = pt(inp["lru_b_r"][0])
    vecs[:, V_LBI:V_LBI + 8] = pt(inp["lru_b_i"][0])
    vecs[:, V_LAM:V_LAM + 8] = pt(inp["lru_lambda"][0])
    vecs[:, V_GLUB:V_GLUB + 4] = pt(inp["glu_b"][0])
    vecs[:, V_EPS] = EPS
    vecs[:, V_ONE] = 1.0
    sh["vecs"] = vecs
    for name, key in (("wrbd", "lru_w_r"), ("wibd", "lru_w_i")):
        w = np.asarray(inp[key][0], dtype=f)
        bd = np.zeros((128, 8, 128), f)
        for h in range(16):
            q, e = h // 2, h % 2
            bd[64 * e:64 * e + 64, q, 64 * e:64 * e + 64] = w[h]
        sh[name] = bd
    def pair(a):
        a = np.asarray(a, dtype=f)
        a = a.reshape((16, 2, 64) + a.shape[2:])
        perm = (1, 2, 0) + tuple(range(3, a.ndim))
        a = a.transpose(perm)
        return np.ascontiguousarray(a.reshape((128, 16) + a.shape[3:]))
    s5sc = np.zeros((128, 3, 16), f)
    s5sc[:, 0] = pair(inp["s5_a_re"][0])
    s5sc[:, 1] = pair(inp["s5_a_im"][0])
    s5sc[:, 2] = pair(np.repeat(np.asarray(inp["s5_log_dt"][0], dtype=f)[:, None], 64, axis=1))
    sh["s5sc"] = s5sc
    s5b = np.zeros((128, 2, 16, 16), f)
    s5b[:, 0] = pair(inp["s5_b_re"][0])
    s5b[:, 1] = pair(inp["s5_b_im"][0])
    sh["s5b"] = s5b
    s5c = np.zeros((128, 2, 16, 16), f)
    s5c[:, 0] = pair(np.asarray(inp["s5_c_re"][0]).transpose(0, 2, 1))
    s5c[:, 1] = pair(np.asarray(inp["s5_c_im"][0]).transpose(0, 2, 1))
    sh["s5c"] = s5c
    dm = np.zeros((8, 16, 32, 8, 16), f)
    d = np.asarray(inp["s5_d"][0], dtype=f).reshape(32, 16)
    for i in range(8):
        for h in range(16):
            dm[i, h, :, i, h] = d[:, h]
    sh["dmat"] = dm.reshape(128, 32, 128)
    cm = np.zeros((8, 16, 8, 16), f)
    for i in range(8):
        cm[i, :, i:, :] = 1.0
    sh["cmask"] = cm.reshape(128, 128)
    bxb = np.asarray(inp["b_in"][0], dtype=f)[2048:2560]
    sh["bxb"] = np.ascontiguousarray(np.broadcast_to(bxb[None, :], (128, 512)))
    sh["ident"] = np.eye(128, dtype=f)
    sh["onesm"] = np.full((128, 128), 1.0 / 1024.0, f)
    return sh


def kernel(**inputs):
    if "nc" not in _NC_CACHE:
        _NC_CACHE["nc"] = build_nc()
    nc = _NC_CACHE["nc"]
    sh = _prep_shared(inputs)
    x = np.asarray(inputs["x"], dtype=np.float32)
    c = np.asarray(inputs["c"], dtype=np.float32)
    in_maps = []
    for b in range(8):
        m = dict(sh)
        m["xin"] = np.ascontiguousarray(x[b])
        m["cvec"] = np.ascontiguousarray(c[b].reshape(8, 128).T)
        in_maps.append(m)
    res = run_bass_kernel_spmd(nc, in_maps, core_ids=list(range(8)))
    return np.stack([np.asarray(r["out"], dtype=np.float32) for r in res.results], axis=0)
```
